# Optimizing a Trainium2 kernel written in Bass

```python
import math
import jax, jax.numpy as jnp
from jax import lax
import numpy as np

D_MODEL = 2048
BATCH = 2
SEQ = 16384
DEPTH = 1

D_RNN = 128 * ((4 * D_MODEL // 3) // 128)
LRU_BLOCKS = 16
LRU_BLOCK = D_RNN // LRU_BLOCKS
LRU_C = 8.0
CONV_REC = 4
ATT_GROUPS = ((128, 1), (512, 4), (2048, 16))
N_GROUPS = 3
HEADS_PER_GROUP = 8
N_ATT_HEADS = N_GROUPS * HEADS_PER_GROUP
HEAD_DIM = 128
ATT_WIDTH = N_ATT_HEADS * HEAD_DIM
ATT_OUT = HEADS_PER_GROUP * HEAD_DIM
Q_BLOCK = 128
D_FF = 256 * ((8 * D_MODEL // 3 + 255) // 256)
CONV_FFN = 3
OFF_GATE_REC = D_RNN
OFF_Q = 2 * D_RNN
OFF_K = OFF_Q + ATT_WIDTH
OFF_V = OFF_K + ATT_WIDTH
OFF_G = OFF_V + ATT_WIDTH
N_IN = OFF_G + 2 * D_MODEL
EPS = 1e-6

kernel_name = "hybrid_rglru_dilated_alibi_convffn"


def rms_norm(x, g):
    xf = x.astype(jnp.float32)
    y = xf * lax.rsqrt(jnp.mean(xf * xf, axis=-1, keepdims=True) + EPS) * g.astype(jnp.float32)
    return y.astype(x.dtype)


def causal_dwconv(x, w, b):
    K, C = w.shape
    y = lax.conv_general_dilated(
        x, w[:, None, :].astype(x.dtype), window_strides=(1,), padding=[(K - 1, 0)],
        dimension_numbers=('NWC', 'WIO', 'NWC'), feature_group_count=C)
    return y + b.astype(x.dtype)


def alibi_slopes(n):
    def pow2_slopes(m):
        start = 2.0 ** (-8.0 / m)
        return [start ** (i + 1) for i in range(m)]
    c = 2 ** int(math.floor(math.log2(n)))
    s = pow2_slopes(c) + pow2_slopes(2 * c)[0::2][: n - c]
    return np.sort(np.asarray(s, np.float32))[::-1].copy()


def rg_lru_branch(x_in, gate_in, conv_w, conv_b, wr, br, wi, bi, lam):
    B, S, _ = x_in.shape
    xf = causal_dwconv(x_in, conv_w, conv_b).astype(jnp.float32)
    xb = xf.reshape(B, S, LRU_BLOCKS, LRU_BLOCK)
    r = jax.nn.sigmoid(jnp.einsum('bsnc,ncd->bsnd', xb, wr.astype(jnp.float32)).reshape(B, S, D_RNN) + br)
    i = jax.nn.sigmoid(jnp.einsum('bsnc,ncd->bsnd', xb, wi.astype(jnp.float32)).reshape(B, S, D_RNN) + bi)
    log_a = -LRU_C * r * jax.nn.softplus(-lam.astype(jnp.float32))
    a = jnp.exp(log_a)
    b = jnp.sqrt(-jnp.expm1(2.0 * log_a)) * (i * xf)

    def combine(left, right):
        a_l, b_l = left
        a_r, b_r = right
        return a_l * a_r, a_r * b_l + b_r

    _, h = lax.associative_scan(combine, (a, b), axis=1)
    return h * jax.nn.gelu(gate_in.astype(jnp.float32))


def dilated_window_attention(q, k, v, slopes, window, dilation):
    B, S, H, hd = q.shape
    span = window // dilation
    chunk = dilation * Q_BLOCK
    s_pad = -(-S // chunk) * chunk
    L = s_pad // dilation
    nb = L // Q_BLOCK
    seq_pad = ((0, 0), (0, s_pad - S), (0, 0), (0, 0))

    def to_sub(t):
        t = jnp.pad(t, seq_pad).reshape(B, L, dilation, H, hd)
        return t.transpose(0, 2, 1, 3, 4)

    def band(t):
        prev = jnp.pad(t, ((0, 0), (0, 0), (Q_BLOCK, 0), (0, 0), (0, 0)))[:, :, :L]
        return jnp.concatenate([prev.reshape(B, dilation, nb, Q_BLOCK, H, hd),
                                t.reshape(B, dilation, nb, Q_BLOCK, H, hd)], axis=3)

    qb = to_sub(q.astype(jnp.float32)).reshape(B, dilation, nb, Q_BLOCK, H, hd)
    kb = band(to_sub(k.astype(jnp.float32)))
    vb = band(to_sub(v.astype(jnp.float32)))

    scores = jnp.einsum('bdnqhe,bdnkhe->bdnhqk', qb, kb) * (hd ** -0.5)
    qi = np.arange(Q_BLOCK)[:, None]
    ki = np.arange(2 * Q_BLOCK)[None, :]
    steps = qi + Q_BLOCK - ki
    in_band = (steps >= 0) & (steps <= span)
    u_key = np.arange(nb)[:, None, None] * Q_BLOCK + ki[None] - Q_BLOCK
    valid = in_band[None] & (u_key >= 0)
    bias = -slopes[:, None, None] * (steps * dilation).astype(np.float32)[None]
    scores = jnp.where(valid[None, None, :, None], scores + bias[None, None, None], -jnp.inf)

    m = jnp.max(scores, axis=-1, keepdims=True)
    p = jnp.exp(scores - m)
    den = jnp.sum(p, axis=-1, keepdims=True)
    out = jnp.einsum('bdnhqk,bdnkhe->bdnhqe', p, vb) / den
    lse = (m + jnp.log(den))[..., 0]
    out = out.transpose(0, 1, 2, 4, 3, 5).reshape(B, dilation, L, H, hd)
    out = out.transpose(0, 2, 1, 3, 4).reshape(B, s_pad, H, hd)[:, :S]
    lse = lse.transpose(0, 1, 2, 4, 3).reshape(B, dilation, L, H)
    lse = lse.transpose(0, 2, 1, 3).reshape(B, s_pad, H)[:, :S]
    return out, lse


def setup_inputs(seed: int = 0) -> dict:
    key = jax.random.key(seed)
    ks = jax.random.split(key, 24)
    f32 = jnp.float32

    def nrm(k, shape, scale):
        return jax.random.normal(k, shape, f32) * scale

    a0 = jax.random.uniform(ks[9], (DEPTH, D_RNN), f32, minval=0.9, maxval=0.999)
    s0 = a0 ** (1.0 / LRU_C)
    lru_lambda = jnp.log(s0) - jnp.log1p(-s0)
    return {
        'x': nrm(ks[0], (BATCH, SEQ, D_MODEL), 1.0),
        'norm1_g': 1.0 + nrm(ks[1], (DEPTH, D_MODEL), 0.02),
        'w_in': nrm(ks[2], (DEPTH, D_MODEL, N_IN), D_MODEL ** -0.5),
        'conv_w': nrm(ks[3], (DEPTH, CONV_REC, D_RNN), CONV_REC ** -0.5),
        'conv_b': nrm(ks[4], (DEPTH, D_RNN), 0.02),
        'lru_wr': nrm(ks[5], (DEPTH, LRU_BLOCKS, LRU_BLOCK, LRU_BLOCK), LRU_BLOCK ** -0.5),
        'lru_br': nrm(ks[6], (DEPTH, D_RNN), 0.02),
        'lru_wi': nrm(ks[7], (DEPTH, LRU_BLOCKS, LRU_BLOCK, LRU_BLOCK), LRU_BLOCK ** -0.5),
        'lru_bi': nrm(ks[8], (DEPTH, D_RNN), 0.02),
        'lru_lambda': lru_lambda,
        'w_rnn_out': nrm(ks[10], (DEPTH, D_RNN, D_MODEL), D_RNN ** -0.5),
        'w_att_out': nrm(ks[11], (DEPTH, ATT_OUT, D_MODEL), ATT_OUT ** -0.5),
        'w_out': nrm(ks[12], (DEPTH, D_MODEL, D_MODEL), D_MODEL ** -0.5),
        'norm2_g': 1.0 + nrm(ks[13], (DEPTH, D_MODEL), 0.02),
        'w_up': nrm(ks[14], (DEPTH, D_MODEL, 2 * D_FF), D_MODEL ** -0.5),
        'ffn_conv_w': nrm(ks[15], (DEPTH, CONV_FFN, D_FF), CONV_FFN ** -0.5),
        'ffn_conv_b': nrm(ks[16], (DEPTH, D_FF), 0.02),
        'w_down': nrm(ks[17], (DEPTH, D_FF, D_MODEL), D_FF ** -0.5),
        'final_g': 1.0 + nrm(ks[18], (D_MODEL,), 0.02),
    }


def reference(x, norm1_g, w_in, conv_w, conv_b, lru_wr, lru_br, lru_wi, lru_bi, lru_lambda,
              w_rnn_out, w_att_out, w_out, norm2_g, w_up, ffn_conv_w, ffn_conv_b, w_down, final_g):
    B, S, _ = x.shape
    slopes = alibi_slopes(N_ATT_HEADS).reshape(N_GROUPS, HEADS_PER_GROUP)
    for l in range(DEPTH):
        h = rms_norm(x, norm1_g[l])
        w = w_in[l]
        x_rec = h @ w[:, :OFF_GATE_REC]
        gate_rec = h @ w[:, OFF_GATE_REC:OFF_Q]
        q = (h @ w[:, OFF_Q:OFF_K]).reshape(B, S, N_GROUPS, HEADS_PER_GROUP, HEAD_DIM)
        k = (h @ w[:, OFF_K:OFF_V]).reshape(B, S, N_GROUPS, HEADS_PER_GROUP, HEAD_DIM)
        v = (h @ w[:, OFF_V:OFF_G]).reshape(B, S, N_GROUPS, HEADS_PER_GROUP, HEAD_DIM)
        gates = jax.nn.sigmoid((h @ w[:, OFF_G:]).astype(jnp.float32))
        g_rec, g_att = gates[..., :D_MODEL], gates[..., D_MODEL:]

        y_rec = rg_lru_branch(x_rec, gate_rec, conv_w[l], conv_b[l], lru_wr[l], lru_br[l],
                              lru_wi[l], lru_bi[l], lru_lambda[l])
        y_a = y_rec.astype(x.dtype) @ w_rnn_out[l]

        outs, lses = [], []
        for g, (window, dilation) in enumerate(ATT_GROUPS):
            o, s = dilated_window_attention(q[:, :, g], k[:, :, g], v[:, :, g], slopes[g], window, dilation)
            outs.append(o)
            lses.append(s)
        wts = jax.nn.softmax(jnp.stack(lses, axis=0), axis=0)
        att = jnp.einsum('gbsh,gbshe->bshe', wts, jnp.stack(outs, axis=0))
        y_b = att.reshape(B, S, ATT_OUT).astype(x.dtype) @ w_att_out[l]

        mixed = g_rec * y_a.astype(jnp.float32) + g_att * y_b.astype(jnp.float32)
        x = x + mixed.astype(x.dtype) @ w_out[l]

        h2 = rms_norm(x, norm2_g[l])
        wu = w_up[l]
        gate = causal_dwconv(h2 @ wu[:, :D_FF], ffn_conv_w[l], ffn_conv_b[l])
        val = h2 @ wu[:, D_FF:]
        x = x + (jax.nn.gelu(gate) * val) @ w_down[l]
    return rms_norm(x, final_g)
```

```python
import math
from contextlib import ExitStack
import numpy as np
import concourse.bass as bass
import concourse.mybir as mybir
from concourse.bass_utils import run_bass_kernel_spmd

F32 = mybir.dt.float32
BF16 = mybir.dt.bfloat16
AF = mybir.ActivationFunctionType
ALU = mybir.AluOpType

D = 2048
KC = 16
DR = 2688
NCT = 21
DFF = 5632
NFT = 44
N_IN = 18688
OFF_GREC = 2688
OFF_Q = 5376
OFF_K = OFF_Q + 3072
OFF_V = OFF_K + 3072
OFF_G = OFF_V + 3072
EPS = 1e-6
NL = 16384
DIL = (1, 4, 16)
SCR0 = 10240
NSCR = NL - SCR0


def alibi_slopes(n):
    def p2(m):
        start = 2.0 ** (-8.0 / m)
        return [start ** (i + 1) for i in range(m)]
    c = 2 ** int(math.floor(math.log2(n)))
    s = p2(c) + p2(2 * c)[0::2][: n - c]
    return np.sort(np.asarray(s, np.float32))[::-1].copy()


def lru_pairs():
    pairs = []
    for j in range(NCT):
        n_lo = (128 * j) // 168
        n_hi = (128 * j + 127) // 168
        k_lo = (168 * n_lo) // 128
        k_hi = (168 * (n_hi + 1) - 1) // 128
        pairs.append(list(range(k_lo, min(k_hi, NCT - 1) + 1)))
    return pairs


PAIRS = lru_pairs()
NPAIR = sum(len(p) for p in PAIRS)


class Buf:
    __slots__ = ("w", "r")

    def __init__(self):
        self.w = {}
        self.r = {}


class Tile:
    def __init__(self, t, sem=None):
        self.t = t
        self.buf = Buf()
        self.dsem = sem
        self.dcount = 0

    def __getitem__(self, k):
        return self.t[k]


class Eng:
    def __init__(self, ctx, e, name, is_pe=False):
        self.ctx = ctx
        self.e = e
        self.name = name
        self.is_pe = is_pe
        self.sem = None
        self.count = 0
        self.waited = {}
        self.own = set()

    def new_epoch(self):
        self.sem = self.ctx.new_sem(self.name)
        self.own.add(id(self.sem))
        self.count = 0


class Ctx:
    EPOCH = 30000

    def __init__(self, nc, es):
        self.nc = nc
        self.es = es
        self.es_sem = es
        self.dtiles = []
        self.nsem = 0
        self.sems = {}
        self.pe = Eng(self, nc.tensor, "pe", True)
        self.act = Eng(self, nc.scalar, "act")
        self.dve = Eng(self, nc.vector, "dve")
        self.pool = Eng(self, nc.gpsimd, "pool")
        self.sp = Eng(self, nc.sync, "sp")
        for e in (self.pe, self.act, self.dve, self.pool, self.sp):
            e.new_epoch()
        self.dbufs = {}
        self.ninst = 0

    def new_sem(self, name):
        self.nsem += 1
        s = self.es_sem.enter_context(self.nc.semaphore(f"{name}_{self.nsem}"))
        self.sems[id(s)] = s
        return s

    def tile(self, name, shape, dtype, dma=False):
        t = self.es.enter_context(self.nc.sbuf_tensor("sb_" + name, list(shape), dtype))
        tl = Tile(t, self.new_sem("d" + name) if dma else None)
        if dma:
            self.dtiles.append(tl)
        return tl

    def barrier(self):
        engs = (self.pe, self.act, self.dve, self.pool, self.sp)
        for E in engs:
            for F in engs:
                if F is not E and F.count > 0:
                    E.e.wait_ge(F.sem, F.count)
                    E.waited[id(F.sem)] = F.count
            for T in self.dtiles:
                if T.dcount > 0:
                    E.e.wait_ge(T.dsem, T.dcount)
                    E.waited[id(T.dsem)] = T.dcount

    def psum(self, name, shape, dtype):
        t = self.es.enter_context(self.nc.psum_tensor("ps_" + name, list(shape), dtype))
        return Tile(t)

    def dbuf(self, key):
        b = self.dbufs.get(key)
        if b is None:
            b = Buf()
            self.dbufs[key] = b
        return b

    @staticmethod
    def _b(x):
        return x.buf if isinstance(x, Tile) else x

    def _deps(self, reads, writes):
        deps = {}

        def add(d):
            for k, v in d.items():
                if deps.get(k, (None, 0))[1] < v[1]:
                    deps[k] = v
        for x in reads:
            add(self._b(x).w)
        for x in writes:
            b = self._b(x)
            add(b.w)
            add(b.r)
        return deps

    def _wait(self, eng, deps):
        for k, (sem, val) in deps.items():
            if eng.is_pe and k in eng.own:
                continue
            if eng.waited.get(k, 0) >= val:
                continue
            eng.e.wait_ge(sem, val)
            eng.waited[k] = val

    def _record(self, reads, writes, sem, val):
        k = id(sem)
        for x in reads:
            b = self._b(x)
            if b.r.get(k, (None, 0))[1] < val:
                b.r[k] = (sem, val)
        for x in writes:
            b = self._b(x)
            if b.r:
                b.w = {}
                b.r = {}
            b.w[k] = (sem, val)

    def op(self, eng, fn, reads=(), writes=()):
        deps = self._deps(reads, writes)
        self._wait(eng, deps)
        if eng.count >= self.EPOCH:
            eng.new_epoch()
        ins = fn(eng.e)
        eng.count += 1
        ins.then_inc(eng.sem, 1)
        self._record(reads, writes, eng.sem, eng.count)
        self.ninst += 1
        return ins

    def dma(self, eng, out, in_, st, reads=(), writes=(), **kw):
        deps = self._deps(reads, writes)
        self._wait(eng, deps)
        ins = eng.e.dma_start(out=out, in_=in_, **kw)
        st.dcount += 16
        ins.then_inc(st.dsem, 16)
        self._record(reads, writes, st.dsem, st.dcount)
        self.ninst += 1
        return ins


def build_program(dbg=False):
    nc = bass.Bass("TRN2", target_bir_lowering=False)

    def din(name, shape, dt=F32):
        return nc.dram_tensor(name, list(shape), dt, kind="ExternalInput").ap()

    def dscr(name, shape, dt=BF16):
        return nc.dram_tensor(name, list(shape), dt, kind="Internal").ap()

    xloc = din("xloc", [NL, D])
    vflag_d = din("vflag", [128, 8])
    w_in = din("w_in", [D, N_IN])
    w_rnn = din("w_rnn", [DR, D])
    w_att = din("w_att", [1024, D])
    w_out = din("w_out", [D, D])
    w_up = din("w_up", [D, 2 * DFF])
    w_down = din("w_down", [DFF, D])
    g1_d = din("g1b", [128, D])
    g2_d = din("g2b", [128, D])
    gf_d = din("gfb", [128, D])
    lrup_d = din("lrup", [128, 8, NCT])
    wrd_d = din("wrd", [128, NPAIR, 128])
    wid_d = din("wid", [128, NPAIR, 128])
    fcw_d = din("fcw", [128, 4, NFT])
    bh_d = din("biash", [128, 24, 256])
    bl_d = din("biasl", [128, 24, 256])
    ident_d = din("ident", [128, 128])
    y = nc.dram_tensor("y", [4096, D], F32, kind="ExternalOutput").ap()

    hT_scr = dscr("hT_scr", [KC, 128, NSCR])
    yr_scr = dscr("yr_scr", [NCT, 128, NSCR])
    at_scr = dscr("at_scr", [8, 128, NSCR])
    qT_scr = dscr("qT_scr", [24, 128, 3, 2048])
    kT_scr = dscr("kT_scr", [24, 128, 4, 2048])
    v_scr = dscr("v_scr", [3, 4, 16, 128, 1024])
    w_rnn_b = dscr("w_rnn_b", [DR, D])
    w_att_b = dscr("w_att_b", [1024, D])
    w_out_b = dscr("w_out_b", [D, D])
    w_up_b = dscr("w_up_b", [D, 2 * DFF])
    w_down_b = dscr("w_down_b", [DFF, D])
    w_g_b = dscr("w_g_b", [D, 2 * D])

    es = ExitStack()
    with es:
        cx = Ctx(nc, es)
        PE, ACT, DVE, POOL, SP = cx.pe, cx.act, cx.dve, cx.pool, cx.sp
        op, dma = cx.op, cx.dma

        vflag = cx.tile("vflag", [128, 8], F32, dma=True)
        lrup = cx.tile("lrup", [128, 8, NCT], F32, dma=True)
        fcw = cx.tile("fcw", [128, 4, NFT], F32, dma=True)
        ident = cx.tile("ident", [128, 128], BF16, dma=True)
        ones = cx.tile("ones", [128, 128], BF16)
        c1 = cx.tile("c1", [128, NCT], F32)
        c2 = cx.tile("c2", [128, NCT], F32)
        state = cx.tile("state", [128, NCT], F32)
        xhalo = cx.tile("xhalo", [128, NCT, 3], F32)
        stat = cx.tile("stat", [128, 4], F32)
        tmpA = cx.tile("tmpA", [128, NCT], F32)
        tmpB = cx.tile("tmpB", [128, NCT], F32)
        tmpC = cx.tile("tmpC", [128, NCT], F32)
        junk = cx.tile("junk", [128, D], BF16)
        xbuf = [cx.tile(f"xbuf{i}", [128, D], F32, dma=True) for i in range(2)]
        hb = cx.tile("hb", [128, D], BF16)
        pus = [cx.psum(f"pu{i}", [128, 1024], F32) for i in range(3)]
        pT = [cx.psum(f"pT{i}", [128, 8, 128], BF16) for i in range(2)]
        pctr = [0]

        def psum_unit():
            pctr[0] += 1
            return pus[pctr[0] % 3]

        dma(SP, vflag[:], vflag_d, vflag, writes=[vflag])
        dma(SP, lrup[:], lrup_d, lrup, writes=[lrup])
        dma(SP, fcw[:], fcw_d, fcw, writes=[fcw])
        dma(POOL, ident[:], ident_d, ident, writes=[ident])
        op(DVE, lambda e: e.memset(ones[:], 1.0), writes=[ones])
        op(DVE, lambda e: e.memset(state[:], 0.0), writes=[state])
        op(DVE, lambda e: e.memset(xhalo[:], 0.0), writes=[xhalo])
        lam = lrup[:, 7, :]
        op(ACT, lambda e: e.activation(out=tmpA[:], in_=lam, func=AF.Exp, scale=-1.0), reads=[lrup], writes=[tmpA])
        op(DVE, lambda e: e.tensor_scalar(out=tmpB[:], in0=tmpA[:], scalar1=1.0e30, scalar2=None, op0=ALU.min), reads=[tmpA], writes=[tmpB])
        op(ACT, lambda e: e.activation(out=tmpB[:], in_=tmpB[:], func=AF.Ln, bias=1.0, scale=1.0), reads=[tmpB], writes=[tmpB])
        op(DVE, lambda e: e.tensor_scalar(out=tmpC[:], in0=tmpA[:], scalar1=1.0, scalar2=None, op0=ALU.min), reads=[tmpA], writes=[tmpC])
        op(DVE, lambda e: e.tensor_scalar(out=c1[:], in0=tmpC[:], scalar1=-0.25, scalar2=1.0 / 3.0, op0=ALU.mult, op1=ALU.add), reads=[tmpC], writes=[c1])
        op(DVE, lambda e: e.tensor_tensor(out=c1[:], in0=c1[:], in1=tmpC[:], op=ALU.mult), reads=[c1, tmpC], writes=[c1])
        op(DVE, lambda e: e.tensor_scalar(out=c1[:], in0=c1[:], scalar1=-1.0, scalar2=0.5, op0=ALU.mult, op1=ALU.add), reads=[c1], writes=[c1])
        op(DVE, lambda e: e.tensor_tensor(out=c1[:], in0=c1[:], in1=tmpC[:], op=ALU.mult), reads=[c1, tmpC], writes=[c1])
        op(DVE, lambda e: e.tensor_scalar(out=c1[:], in0=c1[:], scalar1=-1.0, scalar2=1.0, op0=ALU.mult, op1=ALU.add), reads=[c1], writes=[c1])
        op(DVE, lambda e: e.tensor_tensor(out=c1[:], in0=c1[:], in1=tmpC[:], op=ALU.mult), reads=[c1, tmpC], writes=[c1])
        op(DVE, lambda e: e.tensor_single_scalar(out=c2[:], in_=tmpA[:], scalar=0.03, op=ALU.is_lt), reads=[tmpA], writes=[c2])
        op(DVE, lambda e: e.tensor_tensor(out=c1[:], in0=c1[:], in1=tmpB[:], op=ALU.subtract), reads=[c1, tmpB], writes=[c1])
        op(DVE, lambda e: e.tensor_tensor(out=c1[:], in0=c1[:], in1=c2[:], op=ALU.mult), reads=[c1, c2], writes=[c1])
        op(DVE, lambda e: e.tensor_tensor(out=tmpB[:], in0=c1[:], in1=tmpB[:], op=ALU.add), reads=[c1, tmpB], writes=[tmpB])
        op(DVE, lambda e: e.tensor_scalar(out=c1[:], in0=tmpB[:], scalar1=-8.0, scalar2=None, op0=ALU.mult), reads=[tmpB], writes=[c1])
        op(DVE, lambda e: e.tensor_scalar(out=c2[:], in0=tmpB[:], scalar1=-16.0, scalar2=None, op0=ALU.mult), reads=[tmpB], writes=[c2])
        hb5 = cx.tile("hb5", [128, NCT], F32)
        hb6 = cx.tile("hb6", [128, NCT], F32)
        c1h = cx.tile("c1h", [128, NCT], F32)
        vfh = cx.tile("vfh", [128, 8], F32)
        op(DVE, lambda e: e.tensor_scalar(out=hb5[:], in0=lrup[:, 5, :], scalar1=0.5, scalar2=None, op0=ALU.mult), reads=[lrup], writes=[hb5])
        op(DVE, lambda e: e.tensor_scalar(out=hb6[:], in0=lrup[:, 6, :], scalar1=0.5, scalar2=None, op0=ALU.mult), reads=[lrup], writes=[hb6])
        op(DVE, lambda e: e.tensor_scalar(out=c1h[:], in0=tmpB[:], scalar1=-4.0, scalar2=None, op0=ALU.mult), reads=[tmpB], writes=[c1h])
        op(DVE, lambda e: e.tensor_scalar(out=vfh[:], in0=vflag[:], scalar1=0.5, scalar2=None, op0=ALU.mult), reads=[vflag], writes=[vfh])

        wcast = Tile(None, cx.new_sem("wcast"))
        cx.dtiles.append(wcast)
        WKEY = cx.dbuf("wcast")
        cast_jobs = []
        for (src, dst_, rows) in ((w_rnn, w_rnn_b, DR), (w_att, w_att_b, 1024), (w_out, w_out_b, D), (w_up, w_up_b, D), (w_down, w_down_b, DFF),
                                  (w_in[:, OFF_G:OFF_G + 2 * D], w_g_b, D)):
            for r0 in range(0, rows, 256):
                r1 = min(rows, r0 + 256)
                cast_jobs.append((dst_[r0:r1, :], src[r0:r1, :]))

        def emit_casts(k):
            for _ in range(k):
                if cast_jobs:
                    o_, i_ = cast_jobs.pop(0)
                    dma(POOL, o_, i_, wcast, writes=[WKEY], max_dma_last_dim=4096)

        def rmsnorm_rows(xt, gt, out_ap, out_tile):
            op(ACT, lambda e: e.activation(out=junk[:], in_=xt[:], func=AF.Square, accum_out=stat[:, 0:1]), reads=[xt], writes=[junk, stat])
            op(ACT, lambda e: e.activation(out=stat[:, 1:2], in_=stat[:, 0:1], func=AF.Sqrt, bias=EPS, scale=1.0 / D), reads=[stat], writes=[stat])
            op(DVE, lambda e: e.reciprocal(out=stat[:, 2:3], in_=stat[:, 1:2]), reads=[stat], writes=[stat])
            op(DVE, lambda e: e.scalar_tensor_tensor(out=out_ap, in0=xt[:], scalar=stat[:, 2:3], in1=gt[:], op0=ALU.mult, op1=ALU.mult),
               reads=[xt, stat, gt], writes=[out_tile])

        def transpose_into(hT, col0):
            for kc in range(KC):
                op(PE, lambda e, kc=kc: e.transpose(out=pT[kc // 8][:, kc % 8, :], in_=hb[:, kc * 128:(kc + 1) * 128], identity=ident[:]),
                   reads=[hb, ident], writes=[pT[kc // 8]])
            op(ACT, lambda e: e.copy(out=hT[:, 0:8, col0:col0 + 128], in_=pT[0][:]), reads=[pT[0]], writes=[hT])
            op(DVE, lambda e: e.tensor_copy(out=hT[:, 8:16, col0:col0 + 128], in_=pT[1][:]), reads=[pT[1]], writes=[hT])

        with ExitStack() as esP:
            cx.es = esP
            G = [cx.tile("G0p", [128, D], F32, dma=True)]
            hT = cx.tile("hT", [128, KC, 1024], BF16, dma=True)
            wrd = cx.tile("wrd", [128, NPAIR, 128], BF16, dma=True)
            wid = cx.tile("wid", [128, NPAIR, 128], BF16, dma=True)
            wring = [cx.tile(f"wring{i}", [128, KC, 128], BF16, dma=True) for i in range(3)]
            wv = [cx.tile(f"wv{i}", [128, KC, 512], BF16, dma=True) for i in range(1)]
            xr = [cx.tile(f"xr{i}", [128, 1027], F32) for i in range(2)]
            xf = [cx.tile(f"xf{i}", [128, 1024], F32) for i in range(4)]
            xf16 = [cx.tile(f"xf16_{i}", [128, 1024], BF16) for i in range(6)]
            gr = [cx.tile(f"gr{i}", [128, 1024], F32) for i in range(2)]
            TRs = [cx.tile(f"TR{i}", [128, 1024], F32) for i in range(2)]
            TIs = [cx.tile(f"TI{i}", [128, 1024], F32) for i in range(2)]
            T2s = [cx.tile(f"T2{i}", [128, 1024], F32) for i in range(2)]
            stg = [cx.tile(f"stg{i}", [128, 1024], BF16, dma=True) for i in range(3)]
            vstg = [cx.tile(f"vstg{i}", [128, 512], BF16, dma=True) for i in range(2)]
            cnt = {"w": 0, "stg": 0, "wv": 0, "vs": 0, "ev": 0}

            dma(POOL, wrd[:], wrd_d, wrd, writes=[wrd], max_dma_last_dim=4096)
            dma(POOL, wid[:], wid_d, wid, writes=[wid], max_dma_last_dim=4096)
            dma(SP, G[0][:], g1_d, G[0], writes=[G[0]])

            def proj(col0):
                cnt["w"] += 1
                wb = wring[cnt["w"] % 3]
                dma(POOL, wb[:], w_in[:, col0:col0 + 128].rearrange("(kc p) n -> p kc n", p=128), wb, writes=[wb])
                pu = psum_unit()
                for q in range(2):
                    for kc in range(KC):
                        op(PE, lambda e, q=q, kc=kc: e.matmul(pu[:, q * 512:(q + 1) * 512], lhsT=wb[:, kc, :], rhs=hT[:, kc, q * 512:(q + 1) * 512],
                                                             start=(kc == 0), stop=(kc == KC - 1)), reads=[wb, hT], writes=[pu])
                return pu

            def evac_blocked(pu, g, scale, dst_ap_fn, dkey):
                cnt["stg"] += 1
                st = stg[cnt["stg"] % 3]
                d = DIL[g]
                cnt["ev"] += 1
                use_act = (cnt["ev"] % 2 == 0)

                def cp(out_ap, in_ap):
                    if use_act:
                        op(ACT, lambda e: e.activation(out=out_ap, in_=in_ap, func=AF.Copy, scale=scale), reads=[pu], writes=[st])
                    else:
                        op(DVE, lambda e: e.tensor_scalar(out=out_ap, in0=in_ap, scalar1=scale, scalar2=None, op0=ALU.mult), reads=[pu], writes=[st])
                if d == 1:
                    cp(st[:], pu[:])
                elif d == 4:
                    for m in range(2):
                        cp(st[:, 512 * m:512 * m + 512].rearrange("k (r p) -> k r p", r=4),
                           pu[:, 512 * m:512 * m + 512].rearrange("k (p r) -> k r p", r=4))
                else:
                    cp(st[:].rearrange("k (r p) -> k r p", r=16), pu[:].rearrange("k (p r) -> k r p", r=16))
                dma(SP, dst_ap_fn(d), st[:] if d < 16 else st[:].rearrange("k (r p) -> k r p", r=16), st, reads=[st], writes=[cx.dbuf(dkey)])

            for u in range(16):
                c, hf = divmod(u, 2)
                L0 = 1024 * u
                kv3 = (c == 4)
                full = (u >= 11)
                qkv = (c >= 5)
                for tt in range(8):
                    xt = xbuf[tt % 2]
                    dma(SP, xt[:], xloc[L0 + 128 * tt:L0 + 128 * tt + 128, :], xt, writes=[xt])
                    rmsnorm_rows(xt, G[0], hb[:], hb)
                    transpose_into(hT, tt * 128)
                if full:
                    dma(SP, hT_scr[:, :, L0 - SCR0:L0 - SCR0 + 1024].rearrange("kc p n -> p kc n"), hT[:], hT, reads=[hT], writes=[cx.dbuf(("hT", u))])

                def gates(j):
                    TR, TI, T2 = TRs[j % 2], TIs[j % 2], T2s[j % 2]
                    pr = psum_unit()
                    base = sum(len(PAIRS[jj]) for jj in range(j))
                    kts = PAIRS[j]
                    for q in range(2):
                        for i, kt in enumerate(kts):
                            op(PE, lambda e, q=q, i=i, kt=kt: e.matmul(pr[:, q * 512:(q + 1) * 512], lhsT=wrd[:, base + i, :], rhs=xf16[kt % 6][:, q * 512:(q + 1) * 512],
                                                                     start=(i == 0), stop=(i == len(kts) - 1)), reads=[wrd, xf16[kt % 6]], writes=[pr])
                    op(ACT, lambda e: e.activation(out=TR[:], in_=pr[:], func=AF.Tanh, bias=hb5[:, j:j + 1], scale=0.5), reads=[pr, hb5], writes=[TR])
                    pi = psum_unit()
                    for q in range(2):
                        for i, kt in enumerate(kts):
                            op(PE, lambda e, q=q, i=i, kt=kt: e.matmul(pi[:, q * 512:(q + 1) * 512], lhsT=wid[:, base + i, :], rhs=xf16[kt % 6][:, q * 512:(q + 1) * 512],
                                                                     start=(i == 0), stop=(i == len(kts) - 1)), reads=[wid, xf16[kt % 6]], writes=[pi])
                    op(ACT, lambda e: e.activation(out=TI[:], in_=pi[:], func=AF.Tanh, bias=hb6[:, j:j + 1], scale=0.5), reads=[pi, hb6], writes=[TI])
                    op(ACT, lambda e: e.activation(out=TR[:], in_=TR[:], func=AF.Exp, bias=c1h[:, j:j + 1], scale=c1h[:, j:j + 1]), reads=[TR, c1h], writes=[TR])
                    op(DVE, lambda e: e.tensor_tensor(out=T2[:], in0=TR[:], in1=TR[:], op=ALU.mult), reads=[TR], writes=[T2])
                    op(ACT, lambda e: e.activation(out=T2[:], in_=T2[:], func=AF.Sqrt, bias=1.0, scale=-1.0), reads=[T2], writes=[T2])
                    xfj = xf[j % 4]
                    op(DVE, lambda e: e.scalar_tensor_tensor(out=TI[:], in0=TI[:], scalar=1.0, in1=xfj[:], op0=ALU.add, op1=ALU.mult),
                       reads=[TI, xfj], writes=[TI])
                    op(DVE, lambda e: e.scalar_tensor_tensor(out=TI[:], in0=TI[:], scalar=vfh[:, c:c + 1], in1=T2[:], op0=ALU.mult, op1=ALU.mult),
                       reads=[TI, vfh, T2], writes=[TI])
                    op(DVE, lambda e: e.tensor_tensor_scan(out=T2[:], data0=TR[:], data1=TI[:], initial=state[:, j:j + 1], op0=ALU.mult, op1=ALU.add),
                       reads=[TR, TI, state], writes=[T2])
                    op(DVE, lambda e: e.tensor_copy(out=state[:, j:j + 1], in_=T2[:, 1023:1024]), reads=[T2], writes=[state])
                    if full:
                        cnt["stg"] += 1
                        st = stg[cnt["stg"] % 3]
                        grj = gr[j % 2]
                        op(DVE, lambda e: e.tensor_tensor(out=st[:], in0=T2[:], in1=grj[:], op=ALU.mult), reads=[T2, grj], writes=[st])
                        dma(SP, yr_scr[j, :, L0 - SCR0:L0 - SCR0 + 1024], st[:], st, reads=[st], writes=[cx.dbuf(("yr", u))])

                def grec(j):
                    pu = proj(OFF_GREC + 128 * j)
                    grj = gr[j % 2]
                    op(ACT, lambda e: e.activation(out=grj[:], in_=pu[:], func=AF.Gelu_apprx_tanh), reads=[pu], writes=[grj])

                for ct in range(NCT):
                    pu = proj(128 * ct)
                    xrt = xr[ct % 2]
                    op(ACT, lambda e: e.copy(out=xrt[:, 3:1027], in_=pu[:]), reads=[pu], writes=[xrt])
                    op(ACT, lambda e: e.copy(out=xrt[:, 0:3], in_=xhalo[:, ct, :]), reads=[xhalo], writes=[xrt])
                    op(ACT, lambda e: e.copy(out=xhalo[:, ct, :], in_=xrt[:, 1024:1027]), reads=[xrt], writes=[xhalo])
                    xft = xf[ct % 4]
                    op(DVE, lambda e: e.tensor_scalar(out=xft[:], in0=xrt[:, 0:1024], scalar1=lrup[:, 0, ct:ct + 1], scalar2=lrup[:, 4, ct:ct + 1], op0=ALU.mult, op1=ALU.add),
                       reads=[xrt, lrup], writes=[xft])
                    for i in range(1, 4):
                        op(DVE, lambda e, i=i: e.scalar_tensor_tensor(out=xft[:], in0=xrt[:, i:i + 1024], scalar=lrup[:, i, ct:ct + 1], in1=xft[:], op0=ALU.mult, op1=ALU.add),
                           reads=[xrt, lrup, xft], writes=[xft])
                    if ct >= 3:
                        if full:
                            grec(ct - 3)
                        gates(ct - 3)
                    x16 = xf16[ct % 6]
                    op(ACT, lambda e: e.copy(out=x16[:], in_=xft[:]), reads=[xft], writes=[x16])
                for jj in (NCT - 3, NCT - 2, NCT - 1):
                    if full:
                        grec(jj)
                    gates(jj)

                emit_casts(7)
                if qkv or kv3:
                    ci = c - 4
                    groups = [2] if kv3 else [0, 1, 2]
                    for g in groups:
                        d = DIL[g]
                        for q in range(2):
                            cnt["wv"] += 1
                            wvt = wv[0]
                            vc0 = OFF_V + g * 1024 + q * 512
                            dma(POOL, wvt[:], w_in[:, vc0:vc0 + 512].rearrange("(kc p) n -> p kc n", p=128), wvt, writes=[wvt])
                            if d == 1:
                                blocks = [(8 * hf + b, 0, 128, slice(128 * b, 128 * b + 128)) for b in range(8)]
                            elif d == 4:
                                blocks = [(4 * (2 * hf + m) + r, 0, 128, slice(512 * m + r, 512 * m + 512, 4)) for m in range(2) for r in range(4)]
                            else:
                                blocks = [(r, 64 * hf, 64, slice(r, 1024, 16)) for r in range(16)]
                            for bidx in range(0, len(blocks), 2):
                                pu = psum_unit()
                                for half in range(2):
                                    bi, p0, M, sl = blocks[bidx + half]
                                    for kc in range(KC):
                                        op(PE, lambda e, kc=kc, half=half, M=M, sl=sl: e.matmul(pu[0:M, half * 512:half * 512 + 512], lhsT=hT[:, kc, sl], rhs=wvt[:, kc, :],
                                                                                           start=(kc == 0), stop=(kc == KC - 1)), reads=[hT, wvt], writes=[pu])
                                for half in range(2):
                                    bi, p0, M, sl = blocks[bidx + half]
                                    cnt["vs"] += 1
                                    vs = vstg[cnt["vs"] % 2]
                                    if cnt["vs"] % 2 == 0:
                                        op(ACT, lambda e, half=half, M=M: e.copy(out=vs[0:M, :], in_=pu[0:M, half * 512:half * 512 + 512]), reads=[pu], writes=[vs])
                                    else:
                                        op(DVE, lambda e, half=half, M=M: e.tensor_copy(out=vs[0:M, :], in_=pu[0:M, half * 512:half * 512 + 512]), reads=[pu], writes=[vs])
                                    dma(SP, v_scr[g, ci, bi, p0:p0 + M, q * 512:q * 512 + 512], vs[0:M, :], vs, reads=[vs], writes=[cx.dbuf(("v", g, ci, bi, q))])
                            for s in range(4 * q, 4 * q + 4):
                                head = g * 8 + s
                                kinds = ["k"] if (kv3 or (u == 10 and g < 2)) else ["q", "k"]
                                for kind in kinds:
                                    col0 = (OFF_Q if kind == "q" else OFF_K) + head * 128
                                    pu = proj(col0)
                                    if kind == "q":
                                        scr, cidx, sc = qT_scr, c - 5, 1.0 / math.sqrt(128.0)
                                    else:
                                        scr, cidx, sc = kT_scr, ci, 1.0

                                    def dst(d, scr=scr, head=head, cidx=cidx):
                                        if d < 16:
                                            return scr[head, :, cidx, hf * 1024:hf * 1024 + 1024]
                                        return scr[head, :, cidx, :].rearrange("k (r p) -> k r p", r=16)[:, :, hf * 64:hf * 64 + 64]
                                    evac_blocked(pu, g, sc, dst, (kind, head, cidx, hf))
            cx.barrier()
        cx.es = es

        with ExitStack() as esA:
            cx.es = esA
            biash = cx.tile("biash", [128, 24, 256], BF16, dma=True)
            biasl = cx.tile("biasl", [128, 24, 256], BF16, dma=True)
            dma(POOL, biash[:], bh_d, biash, writes=[biash], max_dma_last_dim=4096)
            dma(POOL, biasl[:], bl_d, biasl, writes=[biasl], max_dma_last_dim=4096)
            Qt = [cx.tile(f"Qt{i}", [128, 2048], BF16, dma=True) for i in range(2)]
            Kt = [cx.tile(f"Kt{i}", [128, 2, 2048], BF16, dma=True) for i in range(2)]
            Vt = [cx.tile(f"Vt{i}", [128, 2, 16, 128], BF16, dma=True) for i in range(2)]
            Pt = [cx.tile(f"Pt{i}", [128, 4, 256], BF16) for i in range(2)]
            accn = cx.tile("accn", [128, 2048], F32)
            accd = cx.tile("accd", [128, 2048], F32)
            astg = [cx.tile(f"astg{i}", [128, 2048], BF16, dma=True) for i in range(2)]
            it = 0
            nonlocal_ctr = [0]
            for c in (5, 6, 7):
                ci = c - 4
                for s in range(8):
                    for g in range(3):
                        head = g * 8 + s
                        d = DIL[g]
                        M_ = 16 // d
                        it += 1
                        qt, kt_, vt = Qt[it % 2], Kt[it % 2], Vt[it % 2]
                        kread = [cx.dbuf(("k", head, cc, h2)) for cc in (ci - 1, ci) for h2 in (0, 1)]
                        qread = [cx.dbuf(("q", head, c - 5, h2)) for h2 in (0, 1)]
                        vread = [cx.dbuf(("v", g, cc, bi, q)) for cc in (ci - 1, ci) for bi in range(16) for q in (s // 4,)]
                        dma(SP, qt[:], qT_scr[head, :, c - 5, :], qt, reads=qread, writes=[qt])
                        dma(SP, kt_[:], kT_scr[head, :, ci - 1:ci + 1, :], kt_, reads=kread, writes=[kt_])
                        dma(SP, vt[:], v_scr[g, ci - 1:ci + 1, :, :, s * 128:s * 128 + 128].rearrange("c b p e -> p c b e"), vt, reads=vread, writes=[vt])
                        batches = [3] if (c == 5 and g < 2) else [0, 1, 2, 3]
                        def S_stage(bt):
                            nonlocal_ctr[0] += 1
                            pt = Pt[nonlocal_ctr[0] % 2]
                            ps = psum_unit()
                            prevs = []
                            for qi in range(4):
                                bi = 4 * bt + qi
                                m, r = divmod(bi, d)
                                if m > 0:
                                    pc, pbi = 1, bi - d
                                else:
                                    pc, pbi = 0, (M_ - 1) * d + r
                                prevs.append((pc, pbi, m))
                                for hh, (kc_, kb_) in enumerate(((pc, pbi), (1, bi))):
                                    o_ = ps[:, qi * 256 + hh * 128:qi * 256 + hh * 128 + 128]
                                    op(PE, lambda e, o_=o_, kc_=kc_, kb_=kb_, bi=bi: e.matmul(o_, lhsT=kt_[:, kc_, kb_ * 128:kb_ * 128 + 128],
                                                                                          rhs=qt[:, bi * 128:bi * 128 + 128], start=True, stop=False), reads=[kt_, qt], writes=[ps])
                                    op(PE, lambda e, o_=o_, hh=hh: e.matmul(o_, lhsT=ident[:], rhs=biash[:, head, hh * 128:hh * 128 + 128], start=False, stop=False),
                                       reads=[ident, biash], writes=[ps])
                                    op(PE, lambda e, o_=o_, hh=hh: e.matmul(o_, lhsT=ident[:], rhs=biasl[:, head, hh * 128:hh * 128 + 128], start=False, stop=True),
                                       reads=[ident, biasl], writes=[ps])
                            op(ACT, lambda e: e.activation(out=pt[:].rearrange("k a b -> k (a b)"), in_=ps[:], func=AF.Exp), reads=[ps], writes=[pt])
                            for qi in range(4):
                                if prevs[qi][0] == 0:
                                    op(DVE, lambda e, qi=qi: e.tensor_scalar(out=pt[:, qi, 0:128], in0=pt[:, qi, 0:128], scalar1=vflag[:, c - 1:c], scalar2=None, op0=ALU.mult),
                                       reads=[pt, vflag], writes=[pt])
                            return (bt, pt, prevs)

                        def PV_stage(st_):
                            bt, pt, prevs = st_
                            po = psum_unit()
                            for qi in range(4):
                                bi = 4 * bt + qi
                                pc, pbi, m = prevs[qi]
                                op(PE, lambda e, qi=qi, pc=pc, pbi=pbi: e.matmul(po[:, qi * 128:qi * 128 + 128], lhsT=vt[:, pc, pbi, :], rhs=pt[:, qi, 0:128], start=True, stop=False),
                                   reads=[vt, pt], writes=[po])
                                op(PE, lambda e, qi=qi, bi=bi: e.matmul(po[:, qi * 128:qi * 128 + 128], lhsT=vt[:, 1, bi, :], rhs=pt[:, qi, 128:256], start=False, stop=True),
                                   reads=[vt, pt], writes=[po])
                                op(PE, lambda e, qi=qi: e.matmul(po[:, 512 + qi * 128:512 + qi * 128 + 128], lhsT=ones[:], rhs=pt[:, qi, 0:128], start=True, stop=False),
                                   reads=[ones, pt], writes=[po])
                                op(PE, lambda e, qi=qi: e.matmul(po[:, 512 + qi * 128:512 + qi * 128 + 128], lhsT=ones[:], rhs=pt[:, qi, 128:256], start=False, stop=True),
                                   reads=[ones, pt], writes=[po])
                            for (acc, off) in ((accn, 0), (accd, 512)):
                                if d == 1:
                                    av = acc[:, bt * 512:bt * 512 + 512]
                                    pv = po[:, off:off + 512]
                                elif d == 4:
                                    av = acc[:, bt * 512:bt * 512 + 512].rearrange("e (p r) -> e r p", r=4)
                                    pv = po[:, off:off + 512].rearrange("e (r p) -> e r p", r=4)
                                else:
                                    av = acc[:].rearrange("e (p r) -> e r p", r=16)[:, 4 * bt:4 * bt + 4, :]
                                    pv = po[:, off:off + 512].rearrange("e (r p) -> e r p", r=4)
                                if g == 0:
                                    if off == 0:
                                        op(ACT, lambda e, av=av, pv=pv: e.copy(out=av, in_=pv), reads=[po], writes=[acc])
                                    else:
                                        op(DVE, lambda e, av=av, pv=pv: e.tensor_copy(out=av, in_=pv), reads=[po], writes=[acc])
                                else:
                                    op(DVE, lambda e, av=av, pv=pv: e.tensor_tensor(out=av, in0=pv, in1=av, op=ALU.add), reads=[po, acc], writes=[acc])

                        pend = None
                        for bt in batches:
                            cur = S_stage(bt)
                            if pend is not None:
                                PV_stage(pend)
                            pend = cur
                        PV_stage(pend)
                    ast = astg[s % 2]
                    op(DVE, lambda e: e.tensor_scalar(out=accd[:], in0=accd[:], scalar1=1e-30, scalar2=None, op0=ALU.add), reads=[accd], writes=[accd])
                    op(DVE, lambda e: e.reciprocal(out=accd[:], in_=accd[:]), reads=[accd], writes=[accd])
                    op(DVE, lambda e: e.tensor_tensor(out=ast[:], in0=accn[:], in1=accd[:], op=ALU.mult), reads=[accn, accd], writes=[ast])
                    dma(SP, at_scr[s, :, (c - 5) * 2048:(c - 5) * 2048 + 2048], ast[:], ast, reads=[ast], writes=[cx.dbuf(("at", c))])
            cx.barrier()
        cx.es = es

        with ExitStack() as esM:
            cx.es = esM
            G = [cx.tile(f"G{i}m", [128, D], F32, dma=True) for i in range(2)]
            hT = cx.tile("hTm", [128, KC, 512], BF16, dma=True)
            R1 = cx.tile("R1", [128, NFT, 512], BF16, dma=True)
            mixT = cx.tile("mixT", [128, KC, 512], BF16)
            xg = cx.tile("xg", [128, 4, D], F32, dma=True)
            wbuf = [cx.tile(f"wbuf{i}", [128, KC, 256], BF16, dma=True) for i in range(4)]
            sm = [cx.tile(f"sm{i}", [128, 512], F32) for i in range(8)]
            ub = [cx.tile(f"ub{i}", [128, 514], F32) for i in range(2)]
            carry = cx.tile("carry", [128, NFT, 2], F32)
            dma(SP, G[0][:], g2_d, G[0], writes=[G[0]])
            dma(SP, G[1][:], gf_d, G[1], writes=[G[1]])
            op(DVE, lambda e: e.memset(carry[:], 0.0), writes=[carry])
            wc = [0]
            smc = [0]

            def wload(src_ap, nk):
                wc[0] += 1
                wb = wbuf[wc[0] % 4]
                dma(POOL, wb[:, 0:nk, :], src_ap.rearrange("(kc p) n -> p kc n", p=128), wb, reads=[WKEY], writes=[wb])
                return wb

            def smt():
                smc[0] += 1
                return sm[smc[0] % 8]

            emit_casts(1000)
            groups = [(12160, 128)] + [(12288 + 512 * i, 512) for i in range(8)]
            for gi, (L0, n) in enumerate(groups):
                ntt = n // 128
                sc0 = L0 - SCR0
                pre = (gi == 0)
                cdeps = [cx.dbuf(("hT", uu)) for uu in range(10, 16)]
                dma(SP, hT[:, :, 0:n], hT_scr[:, :, sc0:sc0 + n].rearrange("kc p n -> p kc n"), hT, reads=cdeps, writes=[hT])
                dma(SP, R1[:, 0:NCT, 0:n], yr_scr[:, :, sc0:sc0 + n].rearrange("kc p n -> p kc n"), R1, reads=[cx.dbuf(("yr", uu)) for uu in range(10, 16)], writes=[R1])
                dma(SP, R1[:, 24:32, 0:n], at_scr[:, :, sc0:sc0 + n].rearrange("kc p n -> p kc n"), R1, reads=[cx.dbuf(("at", cc)) for cc in (5, 6, 7)], writes=[R1])
                dma(SP, xg[:, 0:ntt, :], xloc[L0:L0 + n, :].rearrange("(t p) f -> p t f", p=128), xg, writes=[xg])
                for fh in range(8):
                    c0 = fh * 256
                    wa = wload(w_rnn_b[0:2048, c0:c0 + 256], 16)
                    wa2 = wload(w_rnn_b[2048:DR, c0:c0 + 256], 5)
                    pu = psum_unit()
                    tya = []
                    for fi in range(2):
                        po_ = pu[:, fi * 512:fi * 512 + n]
                        for kc in range(NCT):
                            wsrc = wa if kc < 16 else wa2
                            kk = kc if kc < 16 else kc - 16
                            op(PE, lambda e, kc=kc, wsrc=wsrc, kk=kk, po_=po_, fi=fi: e.matmul(po_, lhsT=wsrc[:, kk, fi * 128:fi * 128 + 128], rhs=R1[:, kc, 0:n],
                                                                                         start=(kc == 0), stop=(kc == NCT - 1)), reads=[wsrc, R1], writes=[pu])
                        t = smt()
                        op(ACT, lambda e, t=t, po_=po_: e.copy(out=t[:, 0:n], in_=po_), reads=[pu], writes=[t])
                        tya.append(t)
                    wb_ = wload(w_att_b[:, c0:c0 + 256], 8)
                    pu = psum_unit()
                    tyb = []
                    for fi in range(2):
                        po_ = pu[:, fi * 512:fi * 512 + n]
                        for kc in range(8):
                            op(PE, lambda e, kc=kc, po_=po_, fi=fi: e.matmul(po_, lhsT=wb_[:, kc, fi * 128:fi * 128 + 128], rhs=R1[:, 24 + kc, 0:n],
                                                                         start=(kc == 0), stop=(kc == 7)), reads=[wb_, R1], writes=[pu])
                        t = smt()
                        op(ACT, lambda e, t=t, po_=po_: e.copy(out=t[:, 0:n], in_=po_), reads=[pu], writes=[t])
                        tyb.append(t)
                    for (tl, goff) in ((tya, 0), (tyb, D)):
                        wg = wload(w_g_b[:, goff + c0:goff + c0 + 256], 16)
                        pu = psum_unit()
                        for fi in range(2):
                            po_ = pu[:, fi * 512:fi * 512 + n]
                            for kc in range(KC):
                                op(PE, lambda e, kc=kc, po_=po_, fi=fi, wg=wg: e.matmul(po_, lhsT=wg[:, kc, fi * 128:fi * 128 + 128], rhs=hT[:, kc, 0:n],
                                                                                    start=(kc == 0), stop=(kc == KC - 1)), reads=[wg, hT], writes=[pu])
                            t = tl[fi]
                            sg = ub[fi % 2]
                            op(ACT, lambda e, sg=sg, po_=po_: e.activation(out=sg[:, 0:n], in_=po_, func=AF.Sigmoid), reads=[pu], writes=[sg])
                            op(DVE, lambda e, t=t, sg=sg: e.tensor_tensor(out=t[:, 0:n], in0=t[:, 0:n], in1=sg[:, 0:n], op=ALU.mult), reads=[t, sg], writes=[t])
                    for fi in range(2):
                        ft = fh * 2 + fi
                        ta, tb = tya[fi], tyb[fi]
                        op(DVE, lambda e, ta=ta, tb=tb, ft=ft: e.tensor_tensor(out=mixT[:, ft, 0:n], in0=ta[:, 0:n], in1=tb[:, 0:n], op=ALU.add), reads=[ta, tb], writes=[mixT])
                for cq in range(8):
                    wo = wload(w_out_b[:, cq * 256:cq * 256 + 256], 16)
                    pu = psum_unit()
                    for tt in range(ntt):
                        po_ = pu[:, tt * 256:tt * 256 + 256]
                        for kc in range(KC):
                            op(PE, lambda e, kc=kc, po_=po_, tt=tt: e.matmul(po_, lhsT=mixT[:, kc, tt * 128:tt * 128 + 128], rhs=wo[:, kc, :],
                                                                         start=(kc == 0), stop=(kc == KC - 1)), reads=[mixT, wo], writes=[pu])
                    op(DVE, lambda e: e.tensor_tensor(out=xg[:, 0:ntt, cq * 256:cq * 256 + 256], in0=pu[:, 0:ntt * 256].rearrange("p (t c) -> p t c", c=256),
                                                      in1=xg[:, 0:ntt, cq * 256:cq * 256 + 256], op=ALU.add), reads=[pu, xg], writes=[xg])
                for tt in range(ntt):
                    op(ACT, lambda e, tt=tt: e.activation(out=junk[:], in_=xg[:, tt, :], func=AF.Square, accum_out=stat[:, 0:1]), reads=[xg], writes=[junk, stat])
                    op(ACT, lambda e: e.activation(out=stat[:, 1:2], in_=stat[:, 0:1], func=AF.Sqrt, bias=EPS, scale=1.0 / D), reads=[stat], writes=[stat])
                    op(DVE, lambda e: e.reciprocal(out=stat[:, 2:3], in_=stat[:, 1:2]), reads=[stat], writes=[stat])
                    op(DVE, lambda e, tt=tt: e.scalar_tensor_tensor(out=hb[:], in0=xg[:, tt, :], scalar=stat[:, 2:3], in1=G[0][:], op0=ALU.mult, op1=ALU.mult),
                       reads=[xg, stat, G[0]], writes=[hb])
                    transpose_into(hT, tt * 128)
                for fq in range(NFT // 2):
                    wg = wload(w_up_b[:, fq * 256:fq * 256 + 256], 16)
                    wvl = wload(w_up_b[:, DFF + fq * 256:DFF + fq * 256 + 256], 16)
                    for fi in range(2):
                        ff = fq * 2 + fi
                        pu = psum_unit()
                        pg = pu[:, 0:n]
                        pv = pu[:, 512:512 + n]
                        for kc in range(KC):
                            op(PE, lambda e, kc=kc, fi=fi: e.matmul(pg, lhsT=wg[:, kc, fi * 128:fi * 128 + 128], rhs=hT[:, kc, 0:n], start=(kc == 0), stop=(kc == KC - 1)),
                               reads=[wg, hT], writes=[pu])
                        for kc in range(KC):
                            op(PE, lambda e, kc=kc, fi=fi: e.matmul(pv, lhsT=wvl[:, kc, fi * 128:fi * 128 + 128], rhs=hT[:, kc, 0:n], start=(kc == 0), stop=(kc == KC - 1)),
                               reads=[wvl, hT], writes=[pu])
                        u_ = ub[ff % 2]
                        op(ACT, lambda e, u_=u_: e.copy(out=u_[:, 2:2 + n], in_=pg), reads=[pu], writes=[u_])
                        op(ACT, lambda e, u_=u_, ff=ff: e.copy(out=u_[:, 0:2], in_=carry[:, ff, :]), reads=[carry], writes=[u_])
                        if pre:
                            op(DVE, lambda e, u_=u_, ff=ff: e.tensor_scalar(out=carry[:, ff, :], in0=u_[:, n:n + 2], scalar1=vflag[:, 5:6], scalar2=None, op0=ALU.mult),
                               reads=[u_, vflag], writes=[carry])
                        else:
                            op(ACT, lambda e, u_=u_, ff=ff: e.copy(out=carry[:, ff, :], in_=u_[:, n:n + 2]), reads=[u_], writes=[carry])
                        t = smt()
                        op(DVE, lambda e, t=t, u_=u_, ff=ff: e.tensor_scalar(out=t[:, 0:n], in0=u_[:, 0:n], scalar1=fcw[:, 0, ff:ff + 1], scalar2=fcw[:, 3, ff:ff + 1], op0=ALU.mult, op1=ALU.add),
                           reads=[u_, fcw], writes=[t])
                        for i in (1, 2):
                            op(DVE, lambda e, t=t, u_=u_, ff=ff, i=i: e.scalar_tensor_tensor(out=t[:, 0:n], in0=u_[:, i:i + n], scalar=fcw[:, i, ff:ff + 1], in1=t[:, 0:n], op0=ALU.mult, op1=ALU.add),
                               reads=[u_, fcw, t], writes=[t])
                        op(ACT, lambda e, t=t: e.activation(out=t[:, 0:n], in_=t[:, 0:n], func=AF.Gelu_apprx_tanh), reads=[t], writes=[t])
                        op(DVE, lambda e, t=t, ff=ff: e.tensor_tensor(out=R1[:, ff, 0:n], in0=pv, in1=t[:, 0:n], op=ALU.mult), reads=[pu, t], writes=[R1])
                if pre:
                    continue
                for cq in range(8):
                    pa, pb = psum_unit(), psum_unit()
                    pouts = [pa[:, 0:256], pa[:, 512:768], pb[:, 0:256], pb[:, 512:768]]
                    pts = [pa, pa, pb, pb]
                    for kg, (k0, nk) in enumerate(((0, 16), (16, 16), (32, 12))):
                        wd = wload(w_down_b[k0 * 128:(k0 + nk) * 128, cq * 256:cq * 256 + 256], nk)
                        for tt in range(4):
                            for kk in range(nk):
                                kc = k0 + kk
                                op(PE, lambda e, kc=kc, kk=kk, tt=tt: e.matmul(pouts[tt], lhsT=R1[:, kc, tt * 128:tt * 128 + 128], rhs=wd[:, kk, :], start=(kc == 0), stop=(kc == NFT - 1)),
                                   reads=[R1, wd], writes=[pts[tt]])
                    for tt in range(4):
                        op(DVE, lambda e, tt=tt: e.tensor_tensor(out=xg[:, tt, cq * 256:cq * 256 + 256], in0=pouts[tt], in1=xg[:, tt, cq * 256:cq * 256 + 256], op=ALU.add),
                           reads=[pts[tt], xg], writes=[xg])
                for tt in range(4):
                    xo = xbuf[tt % 2]
                    op(ACT, lambda e, tt=tt: e.activation(out=junk[:], in_=xg[:, tt, :], func=AF.Square, accum_out=stat[:, 0:1]), reads=[xg], writes=[junk, stat])
                    op(ACT, lambda e: e.activation(out=stat[:, 1:2], in_=stat[:, 0:1], func=AF.Sqrt, bias=EPS, scale=1.0 / D), reads=[stat], writes=[stat])
                    op(DVE, lambda e: e.reciprocal(out=stat[:, 2:3], in_=stat[:, 1:2]), reads=[stat], writes=[stat])
                    op(DVE, lambda e, tt=tt, xo=xo: e.scalar_tensor_tensor(out=xo[:], in0=xg[:, tt, :], scalar=stat[:, 2:3], in1=G[1][:], op0=ALU.mult, op1=ALU.mult),
                       reads=[xg, stat, G[1]], writes=[xo])
                    r0 = L0 - 12288 + tt * 128
                    dma(SP, y[r0:r0 + 128, :], xo[:], xo, reads=[xo], writes=[cx.dbuf("y")])
            for xo in xbuf:
                SP.e.wait_ge(xo.dsem, xo.dcount)
        cx.es = es
    build_program.ninst = cx.ninst
    return nc


_CACHE = {}


def host_consts(inputs):
    f = np.float32
    slopes = alibi_slopes(24)
    k = np.arange(128)[:, None]
    q = np.arange(128)[None, :]
    import ml_dtypes
    bias = np.zeros((128, 24, 256), np.float64)
    for h in range(24):
        dd = DIL[h // 8]
        sl = float(slopes[h])
        bias[:, h, 0:128] = np.where(k >= q, -sl * dd * (q + 128 - k), -30000.0)
        bias[:, h, 128:256] = np.where(k <= q, -sl * dd * (q - k), -30000.0)
    biash = bias.astype(f).astype(ml_dtypes.bfloat16).astype(f)
    biasl = (bias - biash).astype(f).astype(ml_dtypes.bfloat16).astype(f)

    def ch(v):
        return np.ascontiguousarray(np.asarray(v, f).reshape(NCT, 128).T)
    lrup = np.zeros((128, 8, NCT), f)
    cwv = np.asarray(inputs["conv_w"][0], f)
    for i in range(4):
        lrup[:, i, :] = ch(cwv[i])
    lrup[:, 4, :] = ch(inputs["conv_b"][0])
    lrup[:, 5, :] = ch(inputs["lru_br"][0])
    lrup[:, 6, :] = ch(inputs["lru_bi"][0])
    lrup[:, 7, :] = ch(inputs["lru_lambda"][0])

    def dense(wb):
        full = np.zeros((DR, DR), f)
        for nb in range(16):
            full[168 * nb:168 * nb + 168, 168 * nb:168 * nb + 168] = wb[nb]
        out = np.zeros((128, NPAIR, 128), f)
        pi = 0
        for j in range(NCT):
            for kt in PAIRS[j]:
                out[:, pi, :] = full[128 * kt:128 * kt + 128, 128 * j:128 * j + 128]
                pi += 1
        return out
    fcw = np.zeros((128, 4, NFT), f)
    fw = np.asarray(inputs["ffn_conv_w"][0], f)
    for i in range(3):
        fcw[:, i, :] = fw[i].reshape(NFT, 128).T
    fcw[:, 3, :] = np.asarray(inputs["ffn_conv_b"][0], f).reshape(NFT, 128).T

    def bc(v):
        return np.ascontiguousarray(np.broadcast_to(np.asarray(v, f).reshape(1, D), (128, D)))
    return {
        "w_in": np.ascontiguousarray(inputs["w_in"][0], f), "w_rnn": np.ascontiguousarray(inputs["w_rnn_out"][0], f),
        "w_att": np.ascontiguousarray(inputs["w_att_out"][0], f), "w_out": np.ascontiguousarray(inputs["w_out"][0], f),
        "w_up": np.ascontiguousarray(inputs["w_up"][0], f), "w_down": np.ascontiguousarray(inputs["w_down"][0], f),
        "g1b": bc(inputs["norm1_g"][0]), "g2b": bc(inputs["norm2_g"][0]), "gfb": bc(inputs["final_g"]),
        "lrup": lrup, "wrd": dense(np.asarray(inputs["lru_wr"][0], f)), "wid": dense(np.asarray(inputs["lru_wi"][0], f)),
        "fcw": fcw, "biash": biash, "biasl": biasl, "ident": np.eye(128, dtype=f),
    }


def core_inputs(x, c, consts):
    b, j = divmod(c, 4)
    s0 = 4096 * j
    lo = s0 - 12288
    xl = np.zeros((NL, D), np.float32)
    if lo < 0:
        xl[-lo:] = x[b, 0:s0 + 4096]
    else:
        xl[:] = x[b, lo:s0 + 4096]
    vf = np.zeros((128, 8), np.float32)
    for cc in range(8):
        vf[:, cc] = 1.0 if 2048 * cc >= 12288 - s0 else 0.0
    m = dict(consts)
    m["xloc"] = xl
    m["vflag"] = vf
    return m


def kernel(**inputs):
    x = np.asarray(inputs["x"], np.float32)
    consts = host_consts(inputs)
    if "nc" not in _CACHE:
        _CACHE["nc"] = build_program()
    nc = _CACHE["nc"]
    in_maps = [core_inputs(x, c, consts) for c in range(8)]
    res = run_bass_kernel_spmd(nc, in_maps, core_ids=list(range(8)))
    out = np.zeros((2, 16384, D), np.float32)
    for c in range(8):
        b, j = divmod(c, 4)
        out[b, 4096 * j:4096 * j + 4096] = res.results[c]["y"]
    return out
```

```python
import math
from contextlib import ExitStack
import numpy as np
import concourse.bass as bass
import concourse.mybir as mybir
from concourse.bass_utils import run_bass_kernel_spmd

F32 = mybir.dt.float32
BF16 = mybir.dt.bfloat16
AF = mybir.ActivationFunctionType
ALU = mybir.AluOpType

D = 2048
KC = 16
DR = 2688
NCT = 21
DFF = 5632
NFT = 44
N_IN = 18688
OFF_GREC = 2688
OFF_Q = 5376
OFF_K = OFF_Q + 3072
OFF_V = OFF_K + 3072
OFF_G = OFF_V + 3072
EPS = 1e-6
NL = 16384
DIL = (1, 4, 16)
SCR0 = 10240
NSCR = NL - SCR0


def alibi_slopes(n):
    def p2(m):
        start = 2.0 ** (-8.0 / m)
        return [start ** (i + 1) for i in range(m)]
    c = 2 ** int(math.floor(math.log2(n)))
    s = p2(c) + p2(2 * c)[0::2][: n - c]
    return np.sort(np.asarray(s, np.float32))[::-1].copy()


def lru_pairs():
    pairs = []
    for j in range(NCT):
        n_lo = (128 * j) // 168
        n_hi = (128 * j + 127) // 168
        k_lo = (168 * n_lo) // 128
        k_hi = (168 * (n_hi + 1) - 1) // 128
        pairs.append(list(range(k_lo, min(k_hi, NCT - 1) + 1)))
    return pairs


PAIRS = lru_pairs()
NPAIR = sum(len(p) for p in PAIRS)


class Buf:
    __slots__ = ("w", "r")

    def __init__(self):
        self.w = {}
        self.r = {}


class Tile:
    def __init__(self, t, sem=None):
        self.t = t
        self.buf = Buf()
        self.dsem = sem
        self.dcount = 0

    def __getitem__(self, k):
        return self.t[k]


class Eng:
    def __init__(self, ctx, e, name, is_pe=False):
        self.ctx = ctx
        self.e = e
        self.name = name
        self.is_pe = is_pe
        self.sem = None
        self.count = 0
        self.waited = {}
        self.own = set()

    def new_epoch(self):
        self.sem = self.ctx.new_sem(self.name)
        self.own.add(id(self.sem))
        self.count = 0


class Ctx:
    EPOCH = 30000

    def __init__(self, nc, es):
        self.nc = nc
        self.es = es
        self.es_sem = es
        self.dtiles = []
        self.nsem = 0
        self.sems = {}
        self.pe = Eng(self, nc.tensor, "pe", True)
        self.act = Eng(self, nc.scalar, "act")
        self.dve = Eng(self, nc.vector, "dve")
        self.pool = Eng(self, nc.gpsimd, "pool")
        self.sp = Eng(self, nc.sync, "sp")
        for e in (self.pe, self.act, self.dve, self.pool, self.sp):
            e.new_epoch()
        self.dbufs = {}
        self.ninst = 0

    def new_sem(self, name):
        self.nsem += 1
        s = self.es_sem.enter_context(self.nc.semaphore(f"{name}_{self.nsem}"))
        self.sems[id(s)] = s
        return s

    def tile(self, name, shape, dtype, dma=False):
        t = self.es.enter_context(self.nc.sbuf_tensor("sb_" + name, list(shape), dtype))
        tl = Tile(t, self.new_sem("d" + name) if dma else None)
        if dma:
            self.dtiles.append(tl)
        return tl

    def barrier(self):
        engs = (self.pe, self.act, self.dve, self.pool, self.sp)
        for E in engs:
            for F in engs:
                if F is not E and F.count > 0:
                    E.e.wait_ge(F.sem, F.count)
                    E.waited[id(F.sem)] = F.count
            for T in self.dtiles:
                if T.dcount > 0:
                    E.e.wait_ge(T.dsem, T.dcount)
                    E.waited[id(T.dsem)] = T.dcount

    def psum(self, name, shape, dtype):
        t = self.es.enter_context(self.nc.psum_tensor("ps_" + name, list(shape), dtype))
        return Tile(t)

    def dbuf(self, key):
        b = self.dbufs.get(key)
        if b is None:
            b = Buf()
            self.dbufs[key] = b
        return b

    @staticmethod
    def _b(x):
        return x.buf if isinstance(x, Tile) else x

    def _deps(self, reads, writes):
        deps = {}

        def add(d):
            for k, v in d.items():
                if deps.get(k, (None, 0))[1] < v[1]:
                    deps[k] = v
        for x in reads:
            add(self._b(x).w)
        for x in writes:
            b = self._b(x)
            add(b.w)
            add(b.r)
        return deps

    def _wait(self, eng, deps):
        for k, (sem, val) in deps.items():
            if eng.is_pe and k in eng.own:
                continue
            if eng.waited.get(k, 0) >= val:
                continue
            eng.e.wait_ge(sem, val)
            eng.waited[k] = val

    def _record(self, reads, writes, sem, val):
        k = id(sem)
        for x in reads:
            b = self._b(x)
            if b.r.get(k, (None, 0))[1] < val:
                b.r[k] = (sem, val)
        for x in writes:
            b = self._b(x)
            if b.r:
                b.w = {}
                b.r = {}
            b.w[k] = (sem, val)

    def op(self, eng, fn, reads=(), writes=()):
        deps = self._deps(reads, writes)
        self._wait(eng, deps)
        if eng.count >= self.EPOCH:
            eng.new_epoch()
        ins = fn(eng.e)
        eng.count += 1
        ins.then_inc(eng.sem, 1)
        self._record(reads, writes, eng.sem, eng.count)
        self.ninst += 1
        return ins

    def dma(self, eng, out, in_, st, reads=(), writes=(), **kw):
        deps = self._deps(reads, writes)
        self._wait(eng, deps)
        ins = eng.e.dma_start(out=out, in_=in_, **kw)
        st.dcount += 16
        ins.then_inc(st.dsem, 16)
        self._record(reads, writes, st.dsem, st.dcount)
        self.ninst += 1
        return ins


def build_program(dbg=False):
    nc = bass.Bass("TRN2", target_bir_lowering=False)

    def din(name, shape, dt=F32):
        return nc.dram_tensor(name, list(shape), dt, kind="ExternalInput").ap()

    def dscr(name, shape, dt=BF16):
        return nc.dram_tensor(name, list(shape), dt, kind="Internal").ap()

    xloc = din("xloc", [NL, D])
    vflag_d = din("vflag", [128, 8])
    w_in = din("w_in", [D, N_IN])
    w_rnn = din("w_rnn", [DR, D])
    w_att = din("w_att", [1024, D])
    w_out = din("w_out", [D, D])
    w_up = din("w_up", [D, 2 * DFF])
    w_down = din("w_down", [DFF, D])
    g1_d = din("g1b", [128, D])
    g2_d = din("g2b", [128, D])
    gf_d = din("gfb", [128, D])
    lrup_d = din("lrup", [128, 8, NCT])
    wrd_d = din("wrd", [128, NPAIR, 128])
    wid_d = din("wid", [128, NPAIR, 128])
    fcw_d = din("fcw", [128, 4, NFT])
    bh_d = din("biash", [128, 24, 256])
    bl_d = din("biasl", [128, 24, 256])
    ident_d = din("ident", [128, 128])
    y = nc.dram_tensor("y", [4096, D], F32, kind="ExternalOutput").ap()

    hT_scr = dscr("hT_scr", [KC, 128, NSCR])
    yr_scr = dscr("yr_scr", [NCT, 128, NSCR])
    at_scr = dscr("at_scr", [8, 128, NSCR])
    qT_scr = dscr("qT_scr", [24, 128, 3, 2048])
    kT_scr = dscr("kT_scr", [24, 128, 4, 2048])
    v_scr = dscr("v_scr", [3, 4, 16, 128, 1024])
    w_rnn_b = dscr("w_rnn_b", [DR, D])
    w_att_b = dscr("w_att_b", [1024, D])
    w_out_b = dscr("w_out_b", [D, D])
    w_up_b = dscr("w_up_b", [D, 2 * DFF])
    w_down_b = dscr("w_down_b", [DFF, D])
    w_g_b = dscr("w_g_b", [D, 2 * D])

    es = ExitStack()
    with es:
        cx = Ctx(nc, es)
        PE, ACT, DVE, POOL, SP = cx.pe, cx.act, cx.dve, cx.pool, cx.sp
        op, dma = cx.op, cx.dma

        vflag = cx.tile("vflag", [128, 8], F32, dma=True)
        lrup = cx.tile("lrup", [128, 8, NCT], F32, dma=True)
        fcw = cx.tile("fcw", [128, 4, NFT], F32, dma=True)
        ident = cx.tile("ident", [128, 128], BF16, dma=True)
        ones = cx.tile("ones", [128, 128], BF16)
        c1 = cx.tile("c1", [128, NCT], F32)
        c2 = cx.tile("c2", [128, NCT], F32)
        state = cx.tile("state", [128, NCT], F32)
        xhalo = cx.tile("xhalo", [128, NCT, 3], F32)
        stat = cx.tile("stat", [128, 4], F32)
        tmpA = cx.tile("tmpA", [128, NCT], F32)
        tmpB = cx.tile("tmpB", [128, NCT], F32)
        tmpC = cx.tile("tmpC", [128, NCT], F32)
        junk = cx.tile("junk", [128, D], BF16)
        xbuf = [cx.tile(f"xbuf{i}", [128, D], F32, dma=True) for i in range(2)]
        hb = cx.tile("hb", [128, D], BF16)
        pus = [cx.psum(f"pu{i}", [128, 1024], F32) for i in range(3)]
        pT = [cx.psum(f"pT{i}", [128, 8, 128], BF16) for i in range(2)]
        pctr = [0]

        def psum_unit():
            pctr[0] += 1
            return pus[pctr[0] % 3]

        dma(SP, vflag[:], vflag_d, vflag, writes=[vflag])
        dma(SP, lrup[:], lrup_d, lrup, writes=[lrup])
        dma(SP, fcw[:], fcw_d, fcw, writes=[fcw])
        dma(POOL, ident[:], ident_d, ident, writes=[ident])
        op(DVE, lambda e: e.memset(ones[:], 1.0), writes=[ones])
        op(DVE, lambda e: e.memset(state[:], 0.0), writes=[state])
        op(DVE, lambda e: e.memset(xhalo[:], 0.0), writes=[xhalo])
        lam = lrup[:, 7, :]
        op(ACT, lambda e: e.activation(out=tmpA[:], in_=lam, func=AF.Exp, scale=-1.0), reads=[lrup], writes=[tmpA])
        op(DVE, lambda e: e.tensor_scalar(out=tmpB[:], in0=tmpA[:], scalar1=1.0e30, scalar2=None, op0=ALU.min), reads=[tmpA], writes=[tmpB])
        op(ACT, lambda e: e.activation(out=tmpB[:], in_=tmpB[:], func=AF.Ln, bias=1.0, scale=1.0), reads=[tmpB], writes=[tmpB])
        op(DVE, lambda e: e.tensor_scalar(out=tmpC[:], in0=tmpA[:], scalar1=1.0, scalar2=None, op0=ALU.min), reads=[tmpA], writes=[tmpC])
        op(DVE, lambda e: e.tensor_scalar(out=c1[:], in0=tmpC[:], scalar1=-0.25, scalar2=1.0 / 3.0, op0=ALU.mult, op1=ALU.add), reads=[tmpC], writes=[c1])
        op(DVE, lambda e: e.tensor_tensor(out=c1[:], in0=c1[:], in1=tmpC[:], op=ALU.mult), reads=[c1, tmpC], writes=[c1])
        op(DVE, lambda e: e.tensor_scalar(out=c1[:], in0=c1[:], scalar1=-1.0, scalar2=0.5, op0=ALU.mult, op1=ALU.add), reads=[c1], writes=[c1])
        op(DVE, lambda e: e.tensor_tensor(out=c1[:], in0=c1[:], in1=tmpC[:], op=ALU.mult), reads=[c1, tmpC], writes=[c1])
        op(DVE, lambda e: e.tensor_scalar(out=c1[:], in0=c1[:], scalar1=-1.0, scalar2=1.0, op0=ALU.mult, op1=ALU.add), reads=[c1], writes=[c1])
        op(DVE, lambda e: e.tensor_tensor(out=c1[:], in0=c1[:], in1=tmpC[:], op=ALU.mult), reads=[c1, tmpC], writes=[c1])
        op(DVE, lambda e: e.tensor_single_scalar(out=c2[:], in_=tmpA[:], scalar=0.03, op=ALU.is_lt), reads=[tmpA], writes=[c2])
        op(DVE, lambda e: e.tensor_tensor(out=c1[:], in0=c1[:], in1=tmpB[:], op=ALU.subtract), reads=[c1, tmpB], writes=[c1])
        op(DVE, lambda e: e.tensor_tensor(out=c1[:], in0=c1[:], in1=c2[:], op=ALU.mult), reads=[c1, c2], writes=[c1])
        op(DVE, lambda e: e.tensor_tensor(out=tmpB[:], in0=c1[:], in1=tmpB[:], op=ALU.add), reads=[c1, tmpB], writes=[tmpB])
        op(DVE, lambda e: e.tensor_scalar(out=c1[:], in0=tmpB[:], scalar1=-8.0, scalar2=None, op0=ALU.mult), reads=[tmpB], writes=[c1])
        op(DVE, lambda e: e.tensor_scalar(out=c2[:], in0=tmpB[:], scalar1=-16.0, scalar2=None, op0=ALU.mult), reads=[tmpB], writes=[c2])
        hb5 = cx.tile("hb5", [128, NCT], F32)
        hb6 = cx.tile("hb6", [128, NCT], F32)
        c1h = cx.tile("c1h", [128, NCT], F32)
        vfh = cx.tile("vfh", [128, 8], F32)
        op(DVE, lambda e: e.tensor_scalar(out=hb5[:], in0=lrup[:, 5, :], scalar1=0.5, scalar2=None, op0=ALU.mult), reads=[lrup], writes=[hb5])
        op(DVE, lambda e: e.tensor_scalar(out=hb6[:], in0=lrup[:, 6, :], scalar1=0.5, scalar2=None, op0=ALU.mult), reads=[lrup], writes=[hb6])
        op(DVE, lambda e: e.tensor_scalar(out=c1h[:], in0=tmpB[:], scalar1=-4.0, scalar2=None, op0=ALU.mult), reads=[tmpB], writes=[c1h])
        op(DVE, lambda e: e.tensor_scalar(out=vfh[:], in0=vflag[:], scalar1=0.5, scalar2=None, op0=ALU.mult), reads=[vflag], writes=[vfh])

        wcast = Tile(None, cx.new_sem("wcast"))
        cx.dtiles.append(wcast)
        WKEY = cx.dbuf("wcast")
        cast_jobs = []
        for (src, dst_, rows) in ((w_rnn, w_rnn_b, DR), (w_att, w_att_b, 1024), (w_out, w_out_b, D), (w_up, w_up_b, D), (w_down, w_down_b, DFF),
                                  (w_in[:, OFF_G:OFF_G + 2 * D], w_g_b, D)):
            for r0 in range(0, rows, 256):
                r1 = min(rows, r0 + 256)
                cast_jobs.append((dst_[r0:r1, :], src[r0:r1, :]))

        def emit_casts(k):
            for _ in range(k):
                if cast_jobs:
                    o_, i_ = cast_jobs.pop(0)
                    dma(POOL, o_, i_, wcast, writes=[WKEY], max_dma_last_dim=4096)

        def rmsnorm_rows(xt, gt, out_ap, out_tile):
            op(ACT, lambda e: e.activation(out=junk[:], in_=xt[:], func=AF.Square, accum_out=stat[:, 0:1]), reads=[xt], writes=[junk, stat])
            op(ACT, lambda e: e.activation(out=stat[:, 1:2], in_=stat[:, 0:1], func=AF.Sqrt, bias=EPS, scale=1.0 / D), reads=[stat], writes=[stat])
            op(DVE, lambda e: e.reciprocal(out=stat[:, 2:3], in_=stat[:, 1:2]), reads=[stat], writes=[stat])
            op(DVE, lambda e: e.scalar_tensor_tensor(out=out_ap, in0=xt[:], scalar=stat[:, 2:3], in1=gt[:], op0=ALU.mult, op1=ALU.mult),
               reads=[xt, stat, gt], writes=[out_tile])

        def transpose_into(hT, col0):
            for kc in range(KC):
                op(PE, lambda e, kc=kc: e.transpose(out=pT[kc // 8][:, kc % 8, :], in_=hb[:, kc * 128:(kc + 1) * 128], identity=ident[:]),
                   reads=[hb, ident], writes=[pT[kc // 8]])
            op(ACT, lambda e: e.copy(out=hT[:, 0:8, col0:col0 + 128], in_=pT[0][:]), reads=[pT[0]], writes=[hT])
            op(DVE, lambda e: e.tensor_copy(out=hT[:, 8:16, col0:col0 + 128], in_=pT[1][:]), reads=[pT[1]], writes=[hT])

        with ExitStack() as esP:
            cx.es = esP
            G = [cx.tile("G0p", [128, D], F32, dma=True)]
            hT = cx.tile("hT", [128, KC, 1024], BF16, dma=True)
            wrd = cx.tile("wrd", [128, NPAIR, 128], BF16, dma=True)
            wid = cx.tile("wid", [128, NPAIR, 128], BF16, dma=True)
            wring = [cx.tile(f"wring{i}", [128, KC, 128], BF16, dma=True) for i in range(3)]
            wv = [cx.tile(f"wv{i}", [128, KC, 512], BF16, dma=True) for i in range(1)]
            xr = [cx.tile(f"xr{i}", [128, 1027], F32) for i in range(2)]
            xf = [cx.tile(f"xf{i}", [128, 1024], F32) for i in range(4)]
            xf16 = [cx.tile(f"xf16_{i}", [128, 1024], BF16) for i in range(6)]
            gr = [cx.tile(f"gr{i}", [128, 1024], F32) for i in range(2)]
            TRs = [cx.tile(f"TR{i}", [128, 1024], F32) for i in range(2)]
            TIs = [cx.tile(f"TI{i}", [128, 1024], F32) for i in range(2)]
            T2s = [cx.tile(f"T2{i}", [128, 1024], F32) for i in range(2)]
            stg = [cx.tile(f"stg{i}", [128, 1024], BF16, dma=True) for i in range(3)]
            vstg = [cx.tile(f"vstg{i}", [128, 512], BF16, dma=True) for i in range(2)]
            cnt = {"w": 0, "stg": 0, "wv": 0, "vs": 0, "ev": 0}

            dma(POOL, wrd[:], wrd_d, wrd, writes=[wrd], max_dma_last_dim=4096)
            dma(POOL, wid[:], wid_d, wid, writes=[wid], max_dma_last_dim=4096)
            dma(SP, G[0][:], g1_d, G[0], writes=[G[0]])

            def proj(col0):
                cnt["w"] += 1
                wb = wring[cnt["w"] % 3]
                dma(POOL, wb[:], w_in[:, col0:col0 + 128].rearrange("(kc p) n -> p kc n", p=128), wb, writes=[wb])
                pu = psum_unit()
                for q in range(2):
                    for kc in range(KC):
                        op(PE, lambda e, q=q, kc=kc: e.matmul(pu[:, q * 512:(q + 1) * 512], lhsT=wb[:, kc, :], rhs=hT[:, kc, q * 512:(q + 1) * 512],
                                                             start=(kc == 0), stop=(kc == KC - 1)), reads=[wb, hT], writes=[pu])
                return pu

            def evac_blocked(pu, g, scale, dst_ap_fn, dkey):
                cnt["stg"] += 1
                st = stg[cnt["stg"] % 3]
                d = DIL[g]
                cnt["ev"] += 1
                use_act = (cnt["ev"] % 2 == 0)

                def cp(out_ap, in_ap):
                    if use_act:
                        op(ACT, lambda e: e.activation(out=out_ap, in_=in_ap, func=AF.Copy, scale=scale), reads=[pu], writes=[st])
                    else:
                        op(DVE, lambda e: e.tensor_scalar(out=out_ap, in0=in_ap, scalar1=scale, scalar2=None, op0=ALU.mult), reads=[pu], writes=[st])
                if d == 1:
                    cp(st[:], pu[:])
                elif d == 4:
                    for m in range(2):
                        cp(st[:, 512 * m:512 * m + 512].rearrange("k (r p) -> k r p", r=4),
                           pu[:, 512 * m:512 * m + 512].rearrange("k (p r) -> k r p", r=4))
                else:
                    cp(st[:].rearrange("k (r p) -> k r p", r=16), pu[:].rearrange("k (p r) -> k r p", r=16))
                dma(SP, dst_ap_fn(d), st[:] if d < 16 else st[:].rearrange("k (r p) -> k r p", r=16), st, reads=[st], writes=[cx.dbuf(dkey)])

            for u in range(16):
                c, hf = divmod(u, 2)
                L0 = 1024 * u
                kv3 = (c == 4)
                full = (u >= 11)
                qkv = (c >= 5)
                for tt in range(8):
                    xt = xbuf[tt % 2]
                    dma(SP, xt[:], xloc[L0 + 128 * tt:L0 + 128 * tt + 128, :], xt, writes=[xt])
                    rmsnorm_rows(xt, G[0], hb[:], hb)
                    transpose_into(hT, tt * 128)
                if full:
                    dma(SP, hT_scr[:, :, L0 - SCR0:L0 - SCR0 + 1024].rearrange("kc p n -> p kc n"), hT[:], hT, reads=[hT], writes=[cx.dbuf(("hT", u))])

                def gates(j):
                    TR, TI, T2 = TRs[j % 2], TIs[j % 2], T2s[j % 2]
                    pr = psum_unit()
                    base = sum(len(PAIRS[jj]) for jj in range(j))
                    kts = PAIRS[j]
                    for q in range(2):
                        for i, kt in enumerate(kts):
                            op(PE, lambda e, q=q, i=i, kt=kt: e.matmul(pr[:, q * 512:(q + 1) * 512], lhsT=wrd[:, base + i, :], rhs=xf16[kt % 6][:, q * 512:(q + 1) * 512],
                                                                     start=(i == 0), stop=(i == len(kts) - 1)), reads=[wrd, xf16[kt % 6]], writes=[pr])
                    op(ACT, lambda e: e.activation(out=TR[:], in_=pr[:], func=AF.Tanh, bias=hb5[:, j:j + 1], scale=0.5), reads=[pr, hb5], writes=[TR])
                    pi = psum_unit()
                    for q in range(2):
                        for i, kt in enumerate(kts):
                            op(PE, lambda e, q=q, i=i, kt=kt: e.matmul(pi[:, q * 512:(q + 1) * 512], lhsT=wid[:, base + i, :], rhs=xf16[kt % 6][:, q * 512:(q + 1) * 512],
                                                                     start=(i == 0), stop=(i == len(kts) - 1)), reads=[wid, xf16[kt % 6]], writes=[pi])
                    op(ACT, lambda e: e.activation(out=TI[:], in_=pi[:], func=AF.Tanh, bias=hb6[:, j:j + 1], scale=0.5), reads=[pi, hb6], writes=[TI])
                    op(ACT, lambda e: e.activation(out=T2[:], in_=TR[:], func=AF.Exp, bias=c1[:, j:j + 1], scale=c1[:, j:j + 1]), reads=[TR, c1], writes=[T2])
                    op(ACT, lambda e: e.activation(out=TR[:], in_=TR[:], func=AF.Exp, bias=c1h[:, j:j + 1], scale=c1h[:, j:j + 1]), reads=[TR, c1h], writes=[TR])
                    op(ACT, lambda e: e.activation(out=T2[:], in_=T2[:], func=AF.Sqrt, bias=1.0, scale=-1.0), reads=[T2], writes=[T2])
                    xfj = xf[j % 4]
                    op(DVE, lambda e: e.scalar_tensor_tensor(out=TI[:], in0=TI[:], scalar=1.0, in1=xfj[:], op0=ALU.add, op1=ALU.mult),
                       reads=[TI, xfj], writes=[TI])
                    op(DVE, lambda e: e.scalar_tensor_tensor(out=TI[:], in0=TI[:], scalar=vfh[:, c:c + 1], in1=T2[:], op0=ALU.mult, op1=ALU.mult),
                       reads=[TI, vfh, T2], writes=[TI])
                    op(DVE, lambda e: e.tensor_tensor_scan(out=T2[:], data0=TR[:], data1=TI[:], initial=state[:, j:j + 1], op0=ALU.mult, op1=ALU.add),
                       reads=[TR, TI, state], writes=[T2])
                    op(DVE, lambda e: e.tensor_copy(out=state[:, j:j + 1], in_=T2[:, 1023:1024]), reads=[T2], writes=[state])
                    if full:
                        cnt["stg"] += 1
                        st = stg[cnt["stg"] % 3]
                        grj = gr[j % 2]
                        op(DVE, lambda e: e.tensor_tensor(out=st[:], in0=T2[:], in1=grj[:], op=ALU.mult), reads=[T2, grj], writes=[st])
                        dma(SP, yr_scr[j, :, L0 - SCR0:L0 - SCR0 + 1024], st[:], st, reads=[st], writes=[cx.dbuf(("yr", u))])

                def grec(j):
                    pu = proj(OFF_GREC + 128 * j)
                    grj = gr[j % 2]
                    op(ACT, lambda e: e.activation(out=grj[:], in_=pu[:], func=AF.Gelu_apprx_tanh), reads=[pu], writes=[grj])

                for ct in range(NCT):
                    pu = proj(128 * ct)
                    xrt = xr[ct % 2]
                    op(ACT, lambda e: e.copy(out=xrt[:, 3:1027], in_=pu[:]), reads=[pu], writes=[xrt])
                    op(ACT, lambda e: e.copy(out=xrt[:, 0:3], in_=xhalo[:, ct, :]), reads=[xhalo], writes=[xrt])
                    op(ACT, lambda e: e.copy(out=xhalo[:, ct, :], in_=xrt[:, 1024:1027]), reads=[xrt], writes=[xhalo])
                    xft = xf[ct % 4]
                    op(DVE, lambda e: e.tensor_scalar(out=xft[:], in0=xrt[:, 0:1024], scalar1=lrup[:, 0, ct:ct + 1], scalar2=lrup[:, 4, ct:ct + 1], op0=ALU.mult, op1=ALU.add),
                       reads=[xrt, lrup], writes=[xft])
                    for i in range(1, 4):
                        op(DVE, lambda e, i=i: e.scalar_tensor_tensor(out=xft[:], in0=xrt[:, i:i + 1024], scalar=lrup[:, i, ct:ct + 1], in1=xft[:], op0=ALU.mult, op1=ALU.add),
                           reads=[xrt, lrup, xft], writes=[xft])
                    if ct >= 3:
                        if full:
                            grec(ct - 3)
                        gates(ct - 3)
                    x16 = xf16[ct % 6]
                    op(ACT, lambda e: e.copy(out=x16[:], in_=xft[:]), reads=[xft], writes=[x16])
                for jj in (NCT - 3, NCT - 2, NCT - 1):
                    if full:
                        grec(jj)
                    gates(jj)

                emit_casts(7)
                if qkv or kv3:
                    ci = c - 4
                    groups = [2] if kv3 else [0, 1, 2]
                    for g in groups:
                        d = DIL[g]
                        for q in range(2):
                            cnt["wv"] += 1
                            wvt = wv[0]
                            vc0 = OFF_V + g * 1024 + q * 512
                            dma(POOL, wvt[:], w_in[:, vc0:vc0 + 512].rearrange("(kc p) n -> p kc n", p=128), wvt, writes=[wvt])
                            if d == 1:
                                blocks = [(8 * hf + b, 0, 128, slice(128 * b, 128 * b + 128)) for b in range(8)]
                            elif d == 4:
                                blocks = [(4 * (2 * hf + m) + r, 0, 128, slice(512 * m + r, 512 * m + 512, 4)) for m in range(2) for r in range(4)]
                            else:
                                blocks = None
                            if blocks is None:
                                for rp in range(0, 8, 2):
                                    pu = psum_unit()
                                    for half in range(2):
                                        r = rp + half
                                        for kc in range(KC):
                                            op(PE, lambda e, kc=kc, half=half, r=r: e.matmul(pu[:, half * 512:half * 512 + 512], lhsT=hT[:, kc, r:1024:8], rhs=wvt[:, kc, :],
                                                                                       start=(kc == 0), stop=(kc == KC - 1)), reads=[hT, wvt], writes=[pu])
                                    for half in range(2):
                                        r = rp + half
                                        cnt["vs"] += 1
                                        vs = vstg[cnt["vs"] % 2]
                                        if cnt["vs"] % 2 == 0:
                                            op(ACT, lambda e, half=half: e.copy(out=vs[:], in_=pu[:, half * 512:half * 512 + 512]), reads=[pu], writes=[vs])
                                        else:
                                            op(DVE, lambda e, half=half: e.tensor_copy(out=vs[:], in_=pu[:, half * 512:half * 512 + 512]), reads=[pu], writes=[vs])
                                        for rr in range(2):
                                            dma(SP, v_scr[g, ci, r + 8 * rr, 64 * hf:64 * hf + 64, q * 512:q * 512 + 512], vs[rr:128:2, :], vs, reads=[vs],
                                                writes=[cx.dbuf(("v", g, ci, r + 8 * rr, q))])
                                blocks = []
                            for bidx in range(0, len(blocks), 2):
                                pu = psum_unit()
                                for half in range(2):
                                    bi, p0, M, sl = blocks[bidx + half]
                                    for kc in range(KC):
                                        op(PE, lambda e, kc=kc, half=half, M=M, sl=sl: e.matmul(pu[0:M, half * 512:half * 512 + 512], lhsT=hT[:, kc, sl], rhs=wvt[:, kc, :],
                                                                                           start=(kc == 0), stop=(kc == KC - 1)), reads=[hT, wvt], writes=[pu])
                                for half in range(2):
                                    bi, p0, M, sl = blocks[bidx + half]
                                    cnt["vs"] += 1
                                    vs = vstg[cnt["vs"] % 2]
                                    if cnt["vs"] % 2 == 0:
                                        op(ACT, lambda e, half=half, M=M: e.copy(out=vs[0:M, :], in_=pu[0:M, half * 512:half * 512 + 512]), reads=[pu], writes=[vs])
                                    else:
                                        op(DVE, lambda e, half=half, M=M: e.tensor_copy(out=vs[0:M, :], in_=pu[0:M, half * 512:half * 512 + 512]), reads=[pu], writes=[vs])
                                    dma(SP, v_scr[g, ci, bi, p0:p0 + M, q * 512:q * 512 + 512], vs[0:M, :], vs, reads=[vs], writes=[cx.dbuf(("v", g, ci, bi, q))])
                            for s in range(4 * q, 4 * q + 4):
                                head = g * 8 + s
                                kinds = ["k"] if (kv3 or (u == 10 and g < 2)) else ["q", "k"]
                                for kind in kinds:
                                    col0 = (OFF_Q if kind == "q" else OFF_K) + head * 128
                                    pu = proj(col0)
                                    if kind == "q":
                                        scr, cidx, sc = qT_scr, c - 5, 1.0 / math.sqrt(128.0)
                                    else:
                                        scr, cidx, sc = kT_scr, ci, 1.0

                                    def dst(d, scr=scr, head=head, cidx=cidx):
                                        if d < 16:
                                            return scr[head, :, cidx, hf * 1024:hf * 1024 + 1024]
                                        return scr[head, :, cidx, :].rearrange("k (r p) -> k r p", r=16)[:, :, hf * 64:hf * 64 + 64]
                                    evac_blocked(pu, g, sc, dst, (kind, head, cidx, hf))
            cx.barrier()
        cx.es = es

        with ExitStack() as esA:
            cx.es = esA
            biash = cx.tile("biash", [128, 24, 256], BF16, dma=True)
            biasl = cx.tile("biasl", [128, 24, 256], BF16, dma=True)
            dma(POOL, biash[:], bh_d, biash, writes=[biash], max_dma_last_dim=4096)
            dma(POOL, biasl[:], bl_d, biasl, writes=[biasl], max_dma_last_dim=4096)
            Qt = [cx.tile(f"Qt{i}", [128, 2048], BF16, dma=True) for i in range(2)]
            Kt = [cx.tile(f"Kt{i}", [128, 2, 2048], BF16, dma=True) for i in range(2)]
            Vt = [cx.tile(f"Vt{i}", [128, 2, 16, 128], BF16, dma=True) for i in range(2)]
            Pt = [cx.tile(f"Pt{i}", [128, 4, 256], BF16) for i in range(2)]
            accn = cx.tile("accn", [128, 2048], F32)
            accd = cx.tile("accd", [128, 2048], F32)
            astg = [cx.tile(f"astg{i}", [128, 2048], BF16, dma=True) for i in range(2)]
            it = 0
            nonlocal_ctr = [0]
            for c in (5, 6, 7):
                ci = c - 4
                for s in range(8):
                    for g in range(3):
                        head = g * 8 + s
                        d = DIL[g]
                        M_ = 16 // d
                        it += 1
                        qt, kt_, vt = Qt[it % 2], Kt[it % 2], Vt[it % 2]
                        kread = [cx.dbuf(("k", head, cc, h2)) for cc in (ci - 1, ci) for h2 in (0, 1)]
                        qread = [cx.dbuf(("q", head, c - 5, h2)) for h2 in (0, 1)]
                        vread = [cx.dbuf(("v", g, cc, bi, q)) for cc in (ci - 1, ci) for bi in range(16) for q in (s // 4,)]
                        dma(SP, qt[:], qT_scr[head, :, c - 5, :], qt, reads=qread, writes=[qt])
                        dma(SP, kt_[:], kT_scr[head, :, ci - 1:ci + 1, :], kt_, reads=kread, writes=[kt_])
                        dma(SP, vt[:], v_scr[g, ci - 1:ci + 1, :, :, s * 128:s * 128 + 128].rearrange("c b p e -> p c b e"), vt, reads=vread, writes=[vt])
                        batches = [3] if (c == 5 and g < 2) else [0, 1, 2, 3]
                        def S_stage(bt):
                            nonlocal_ctr[0] += 1
                            pt = Pt[nonlocal_ctr[0] % 2]
                            ps = psum_unit()
                            prevs = []
                            for qi in range(4):
                                bi = 4 * bt + qi
                                m, r = divmod(bi, d)
                                if m > 0:
                                    pc, pbi = 1, bi - d
                                else:
                                    pc, pbi = 0, (M_ - 1) * d + r
                                prevs.append((pc, pbi, m))
                                for hh, (kc_, kb_) in enumerate(((pc, pbi), (1, bi))):
                                    o_ = ps[:, qi * 256 + hh * 128:qi * 256 + hh * 128 + 128]
                                    op(PE, lambda e, o_=o_, kc_=kc_, kb_=kb_, bi=bi: e.matmul(o_, lhsT=kt_[:, kc_, kb_ * 128:kb_ * 128 + 128],
                                                                                          rhs=qt[:, bi * 128:bi * 128 + 128], start=True, stop=False), reads=[kt_, qt], writes=[ps])
                                    op(PE, lambda e, o_=o_, hh=hh: e.matmul(o_, lhsT=ident[:], rhs=biash[:, head, hh * 128:hh * 128 + 128], start=False, stop=False),
                                       reads=[ident, biash], writes=[ps])
                                    op(PE, lambda e, o_=o_, hh=hh: e.matmul(o_, lhsT=ident[:], rhs=biasl[:, head, hh * 128:hh * 128 + 128], start=False, stop=True),
                                       reads=[ident, biasl], writes=[ps])
                            op(ACT, lambda e: e.activation(out=pt[:].rearrange("k a b -> k (a b)"), in_=ps[:], func=AF.Exp), reads=[ps], writes=[pt])
                            for qi in range(4):
                                if prevs[qi][0] == 0:
                                    op(DVE, lambda e, qi=qi: e.tensor_scalar(out=pt[:, qi, 0:128], in0=pt[:, qi, 0:128], scalar1=vflag[:, c - 1:c], scalar2=None, op0=ALU.mult),
                                       reads=[pt, vflag], writes=[pt])
                            return (bt, pt, prevs)

                        def PV_stage(st_):
                            bt, pt, prevs = st_
                            po = psum_unit()
                            for qi in range(4):
                                bi = 4 * bt + qi
                                pc, pbi, m = prevs[qi]
                                op(PE, lambda e, qi=qi, pc=pc, pbi=pbi: e.matmul(po[:, qi * 128:qi * 128 + 128], lhsT=vt[:, pc, pbi, :], rhs=pt[:, qi, 0:128], start=True, stop=False),
                                   reads=[vt, pt], writes=[po])
                                op(PE, lambda e, qi=qi, bi=bi: e.matmul(po[:, qi * 128:qi * 128 + 128], lhsT=vt[:, 1, bi, :], rhs=pt[:, qi, 128:256], start=False, stop=True),
                                   reads=[vt, pt], writes=[po])
                                op(PE, lambda e, qi=qi: e.matmul(po[:, 512 + qi * 128:512 + qi * 128 + 128], lhsT=ones[:], rhs=pt[:, qi, 0:128], start=True, stop=False),
                                   reads=[ones, pt], writes=[po])
                                op(PE, lambda e, qi=qi: e.matmul(po[:, 512 + qi * 128:512 + qi * 128 + 128], lhsT=ones[:], rhs=pt[:, qi, 128:256], start=False, stop=True),
                                   reads=[ones, pt], writes=[po])
                            for (acc, off) in ((accn, 0), (accd, 512)):
                                if d == 1:
                                    av = acc[:, bt * 512:bt * 512 + 512]
                                    pv = po[:, off:off + 512]
                                elif d == 4:
                                    av = acc[:, bt * 512:bt * 512 + 512].rearrange("e (p r) -> e r p", r=4)
                                    pv = po[:, off:off + 512].rearrange("e (r p) -> e r p", r=4)
                                else:
                                    av = acc[:].rearrange("e (p r) -> e r p", r=16)[:, 4 * bt:4 * bt + 4, :]
                                    pv = po[:, off:off + 512].rearrange("e (r p) -> e r p", r=4)
                                if g == 0:
                                    if off == 0:
                                        op(ACT, lambda e, av=av, pv=pv: e.copy(out=av, in_=pv), reads=[po], writes=[acc])
                                    else:
                                        op(DVE, lambda e, av=av, pv=pv: e.tensor_copy(out=av, in_=pv), reads=[po], writes=[acc])
                                else:
                                    op(DVE, lambda e, av=av, pv=pv: e.tensor_tensor(out=av, in0=pv, in1=av, op=ALU.add), reads=[po, acc], writes=[acc])

                        pend = None
                        for bt in batches:
                            cur = S_stage(bt)
                            if pend is not None:
                                PV_stage(pend)
                            pend = cur
                        PV_stage(pend)
                    ast = astg[s % 2]
                    op(DVE, lambda e: e.tensor_scalar(out=accd[:], in0=accd[:], scalar1=1e-30, scalar2=None, op0=ALU.add), reads=[accd], writes=[accd])
                    op(DVE, lambda e: e.reciprocal(out=accd[:], in_=accd[:]), reads=[accd], writes=[accd])
                    op(DVE, lambda e: e.tensor_tensor(out=ast[:], in0=accn[:], in1=accd[:], op=ALU.mult), reads=[accn, accd], writes=[ast])
                    dma(SP, at_scr[s, :, (c - 5) * 2048:(c - 5) * 2048 + 2048], ast[:], ast, reads=[ast], writes=[cx.dbuf(("at", c))])
            cx.barrier()
        cx.es = es

        with ExitStack() as esM:
            cx.es = esM
            G = [cx.tile(f"G{i}m", [128, D], F32, dma=True) for i in range(2)]
            hT = cx.tile("hTm", [128, KC, 512], BF16, dma=True)
            R1 = cx.tile("R1", [128, NFT, 512], BF16, dma=True)
            mixT = cx.tile("mixT", [128, KC, 512], BF16)
            xg = cx.tile("xg", [128, 4, D], F32, dma=True)
            wbuf = [cx.tile(f"wbuf{i}", [128, KC, 256], BF16, dma=True) for i in range(4)]
            sm = [cx.tile(f"sm{i}", [128, 512], F32) for i in range(8)]
            ub = [cx.tile(f"ub{i}", [128, 514], F32) for i in range(2)]
            carry = cx.tile("carry", [128, NFT, 2], F32)
            dma(SP, G[0][:], g2_d, G[0], writes=[G[0]])
            dma(SP, G[1][:], gf_d, G[1], writes=[G[1]])
            op(DVE, lambda e: e.memset(carry[:], 0.0), writes=[carry])
            wc = [0]
            smc = [0]

            def wload(src_ap, nk):
                wc[0] += 1
                wb = wbuf[wc[0] % 4]
                dma(POOL, wb[:, 0:nk, :], src_ap.rearrange("(kc p) n -> p kc n", p=128), wb, reads=[WKEY], writes=[wb])
                return wb

            def smt():
                smc[0] += 1
                return sm[smc[0] % 8]

            emit_casts(1000)
            groups = [(12160, 128)] + [(12288 + 512 * i, 512) for i in range(8)]
            for gi, (L0, n) in enumerate(groups):
                ntt = n // 128
                sc0 = L0 - SCR0
                pre = (gi == 0)
                cdeps = [cx.dbuf(("hT", uu)) for uu in range(10, 16)]
                dma(SP, hT[:, :, 0:n], hT_scr[:, :, sc0:sc0 + n].rearrange("kc p n -> p kc n"), hT, reads=cdeps, writes=[hT])
                dma(SP, R1[:, 0:NCT, 0:n], yr_scr[:, :, sc0:sc0 + n].rearrange("kc p n -> p kc n"), R1, reads=[cx.dbuf(("yr", uu)) for uu in range(10, 16)], writes=[R1])
                dma(SP, R1[:, 24:32, 0:n], at_scr[:, :, sc0:sc0 + n].rearrange("kc p n -> p kc n"), R1, reads=[cx.dbuf(("at", cc)) for cc in (5, 6, 7)], writes=[R1])
                dma(SP, xg[:, 0:ntt, :], xloc[L0:L0 + n, :].rearrange("(t p) f -> p t f", p=128), xg, writes=[xg])
                for fh in range(8):
                    c0 = fh * 256
                    wa = wload(w_rnn_b[0:2048, c0:c0 + 256], 16)
                    wa2 = wload(w_rnn_b[2048:DR, c0:c0 + 256], 5)
                    pu = psum_unit()
                    tya = []
                    for fi in range(2):
                        po_ = pu[:, fi * 512:fi * 512 + n]
                        for kc in range(NCT):
                            wsrc = wa if kc < 16 else wa2
                            kk = kc if kc < 16 else kc - 16
                            op(PE, lambda e, kc=kc, wsrc=wsrc, kk=kk, po_=po_, fi=fi: e.matmul(po_, lhsT=wsrc[:, kk, fi * 128:fi * 128 + 128], rhs=R1[:, kc, 0:n],
                                                                                         start=(kc == 0), stop=(kc == NCT - 1)), reads=[wsrc, R1], writes=[pu])
                        t = smt()
                        op(ACT, lambda e, t=t, po_=po_: e.copy(out=t[:, 0:n], in_=po_), reads=[pu], writes=[t])
                        tya.append(t)
                    wb_ = wload(w_att_b[:, c0:c0 + 256], 8)
                    pu = psum_unit()
                    tyb = []
                    for fi in range(2):
                        po_ = pu[:, fi * 512:fi * 512 + n]
                        for kc in range(8):
                            op(PE, lambda e, kc=kc, po_=po_, fi=fi: e.matmul(po_, lhsT=wb_[:, kc, fi * 128:fi * 128 + 128], rhs=R1[:, 24 + kc, 0:n],
                                                                         start=(kc == 0), stop=(kc == 7)), reads=[wb_, R1], writes=[pu])
                        t = smt()
                        op(ACT, lambda e, t=t, po_=po_: e.copy(out=t[:, 0:n], in_=po_), reads=[pu], writes=[t])
                        tyb.append(t)
                    for (tl, goff) in ((tya, 0), (tyb, D)):
                        wg = wload(w_g_b[:, goff + c0:goff + c0 + 256], 16)
                        pu = psum_unit()
                        for fi in range(2):
                            po_ = pu[:, fi * 512:fi * 512 + n]
                            for kc in range(KC):
                                op(PE, lambda e, kc=kc, po_=po_, fi=fi, wg=wg: e.matmul(po_, lhsT=wg[:, kc, fi * 128:fi * 128 + 128], rhs=hT[:, kc, 0:n],
                                                                                    start=(kc == 0), stop=(kc == KC - 1)), reads=[wg, hT], writes=[pu])
                            t = tl[fi]
                            sg = ub[fi % 2]
                            op(ACT, lambda e, sg=sg, po_=po_: e.activation(out=sg[:, 0:n], in_=po_, func=AF.Sigmoid), reads=[pu], writes=[sg])
                            op(DVE, lambda e, t=t, sg=sg: e.tensor_tensor(out=t[:, 0:n], in0=t[:, 0:n], in1=sg[:, 0:n], op=ALU.mult), reads=[t, sg], writes=[t])
                    for fi in range(2):
                        ft = fh * 2 + fi
                        ta, tb = tya[fi], tyb[fi]
                        op(DVE, lambda e, ta=ta, tb=tb, ft=ft: e.tensor_tensor(out=mixT[:, ft, 0:n], in0=ta[:, 0:n], in1=tb[:, 0:n], op=ALU.add), reads=[ta, tb], writes=[mixT])
                for cq in range(8):
                    wo = wload(w_out_b[:, cq * 256:cq * 256 + 256], 16)
                    pu = psum_unit()
                    for tt in range(ntt):
                        po_ = pu[:, tt * 256:tt * 256 + 256]
                        for kc in range(KC):
                            op(PE, lambda e, kc=kc, po_=po_, tt=tt: e.matmul(po_, lhsT=mixT[:, kc, tt * 128:tt * 128 + 128], rhs=wo[:, kc, :],
                                                                         start=(kc == 0), stop=(kc == KC - 1)), reads=[mixT, wo], writes=[pu])
                    op(DVE, lambda e: e.tensor_tensor(out=xg[:, 0:ntt, cq * 256:cq * 256 + 256], in0=pu[:, 0:ntt * 256].rearrange("p (t c) -> p t c", c=256),
                                                      in1=xg[:, 0:ntt, cq * 256:cq * 256 + 256], op=ALU.add), reads=[pu, xg], writes=[xg])
                for tt in range(ntt):
                    op(ACT, lambda e, tt=tt: e.activation(out=junk[:], in_=xg[:, tt, :], func=AF.Square, accum_out=stat[:, 0:1]), reads=[xg], writes=[junk, stat])
                    op(ACT, lambda e: e.activation(out=stat[:, 1:2], in_=stat[:, 0:1], func=AF.Sqrt, bias=EPS, scale=1.0 / D), reads=[stat], writes=[stat])
                    op(DVE, lambda e: e.reciprocal(out=stat[:, 2:3], in_=stat[:, 1:2]), reads=[stat], writes=[stat])
                    op(DVE, lambda e, tt=tt: e.scalar_tensor_tensor(out=hb[:], in0=xg[:, tt, :], scalar=stat[:, 2:3], in1=G[0][:], op0=ALU.mult, op1=ALU.mult),
                       reads=[xg, stat, G[0]], writes=[hb])
                    transpose_into(hT, tt * 128)
                for fq in range(NFT // 2):
                    wg = wload(w_up_b[:, fq * 256:fq * 256 + 256], 16)
                    wvl = wload(w_up_b[:, DFF + fq * 256:DFF + fq * 256 + 256], 16)
                    for fi in range(2):
                        ff = fq * 2 + fi
                        pu = psum_unit()
                        pg = pu[:, 0:n]
                        pv = pu[:, 512:512 + n]
                        for kc in range(KC):
                            op(PE, lambda e, kc=kc, fi=fi: e.matmul(pg, lhsT=wg[:, kc, fi * 128:fi * 128 + 128], rhs=hT[:, kc, 0:n], start=(kc == 0), stop=(kc == KC - 1)),
                               reads=[wg, hT], writes=[pu])
                        for kc in range(KC):
                            op(PE, lambda e, kc=kc, fi=fi: e.matmul(pv, lhsT=wvl[:, kc, fi * 128:fi * 128 + 128], rhs=hT[:, kc, 0:n], start=(kc == 0), stop=(kc == KC - 1)),
                               reads=[wvl, hT], writes=[pu])
                        u_ = ub[ff % 2]
                        op(ACT, lambda e, u_=u_: e.copy(out=u_[:, 2:2 + n], in_=pg), reads=[pu], writes=[u_])
                        op(ACT, lambda e, u_=u_, ff=ff: e.copy(out=u_[:, 0:2], in_=carry[:, ff, :]), reads=[carry], writes=[u_])
                        if pre:
                            op(DVE, lambda e, u_=u_, ff=ff: e.tensor_scalar(out=carry[:, ff, :], in0=u_[:, n:n + 2], scalar1=vflag[:, 5:6], scalar2=None, op0=ALU.mult),
                               reads=[u_, vflag], writes=[carry])
                        else:
                            op(ACT, lambda e, u_=u_, ff=ff: e.copy(out=carry[:, ff, :], in_=u_[:, n:n + 2]), reads=[u_], writes=[carry])
                        t = smt()
                        op(DVE, lambda e, t=t, u_=u_, ff=ff: e.tensor_scalar(out=t[:, 0:n], in0=u_[:, 0:n], scalar1=fcw[:, 0, ff:ff + 1], scalar2=fcw[:, 3, ff:ff + 1], op0=ALU.mult, op1=ALU.add),
                           reads=[u_, fcw], writes=[t])
                        for i in (1, 2):
                            op(DVE, lambda e, t=t, u_=u_, ff=ff, i=i: e.scalar_tensor_tensor(out=t[:, 0:n], in0=u_[:, i:i + n], scalar=fcw[:, i, ff:ff + 1], in1=t[:, 0:n], op0=ALU.mult, op1=ALU.add),
                               reads=[u_, fcw, t], writes=[t])
                        op(ACT, lambda e, t=t: e.activation(out=t[:, 0:n], in_=t[:, 0:n], func=AF.Gelu_apprx_tanh), reads=[t], writes=[t])
                        op(DVE, lambda e, t=t, ff=ff: e.tensor_tensor(out=R1[:, ff, 0:n], in0=pv, in1=t[:, 0:n], op=ALU.mult), reads=[pu, t], writes=[R1])
                if pre:
                    continue
                for cq in range(8):
                    pa, pb = psum_unit(), psum_unit()
                    pouts = [pa[:, 0:256], pa[:, 512:768], pb[:, 0:256], pb[:, 512:768]]
                    pts = [pa, pa, pb, pb]
                    for kg, (k0, nk) in enumerate(((0, 16), (16, 16), (32, 12))):
                        wd = wload(w_down_b[k0 * 128:(k0 + nk) * 128, cq * 256:cq * 256 + 256], nk)
                        for tt in range(4):
                            for kk in range(nk):
                                kc = k0 + kk
                                op(PE, lambda e, kc=kc, kk=kk, tt=tt: e.matmul(pouts[tt], lhsT=R1[:, kc, tt * 128:tt * 128 + 128], rhs=wd[:, kk, :], start=(kc == 0), stop=(kc == NFT - 1)),
                                   reads=[R1, wd], writes=[pts[tt]])
                    for tt in range(4):
                        op(DVE, lambda e, tt=tt: e.tensor_tensor(out=xg[:, tt, cq * 256:cq * 256 + 256], in0=pouts[tt], in1=xg[:, tt, cq * 256:cq * 256 + 256], op=ALU.add),
                           reads=[pts[tt], xg], writes=[xg])
                for tt in range(4):
                    xo = xbuf[tt % 2]
                    op(ACT, lambda e, tt=tt: e.activation(out=junk[:], in_=xg[:, tt, :], func=AF.Square, accum_out=stat[:, 0:1]), reads=[xg], writes=[junk, stat])
                    op(ACT, lambda e: e.activation(out=stat[:, 1:2], in_=stat[:, 0:1], func=AF.Sqrt, bias=EPS, scale=1.0 / D), reads=[stat], writes=[stat])
                    op(DVE, lambda e: e.reciprocal(out=stat[:, 2:3], in_=stat[:, 1:2]), reads=[stat], writes=[stat])
                    op(DVE, lambda e, tt=tt, xo=xo: e.scalar_tensor_tensor(out=xo[:], in0=xg[:, tt, :], scalar=stat[:, 2:3], in1=G[1][:], op0=ALU.mult, op1=ALU.mult),
                       reads=[xg, stat, G[1]], writes=[xo])
                    r0 = L0 - 12288 + tt * 128
                    dma(SP, y[r0:r0 + 128, :], xo[:], xo, reads=[xo], writes=[cx.dbuf("y")])
            for xo in xbuf:
                SP.e.wait_ge(xo.dsem, xo.dcount)
        cx.es = es
    build_program.ninst = cx.ninst
    return nc


_CACHE = {}


def host_consts(inputs):
    f = np.float32
    slopes = alibi_slopes(24)
    k = np.arange(128)[:, None]
    q = np.arange(128)[None, :]
    import ml_dtypes
    bias = np.zeros((128, 24, 256), np.float64)
    for h in range(24):
        dd = DIL[h // 8]
        sl = float(slopes[h])
        bias[:, h, 0:128] = np.where(k >= q, -sl * dd * (q + 128 - k), -30000.0)
        bias[:, h, 128:256] = np.where(k <= q, -sl * dd * (q - k), -30000.0)
    biash = bias.astype(f).astype(ml_dtypes.bfloat16).astype(f)
    biasl = (bias - biash).astype(f).astype(ml_dtypes.bfloat16).astype(f)

    def ch(v):
        return np.ascontiguousarray(np.asarray(v, f).reshape(NCT, 128).T)
    lrup = np.zeros((128, 8, NCT), f)
    cwv = np.asarray(inputs["conv_w"][0], f)
    for i in range(4):
        lrup[:, i, :] = ch(cwv[i])
    lrup[:, 4, :] = ch(inputs["conv_b"][0])
    lrup[:, 5, :] = ch(inputs["lru_br"][0])
    lrup[:, 6, :] = ch(inputs["lru_bi"][0])
    lrup[:, 7, :] = ch(inputs["lru_lambda"][0])

    def dense(wb):
        full = np.zeros((DR, DR), f)
        for nb in range(16):
            full[168 * nb:168 * nb + 168, 168 * nb:168 * nb + 168] = wb[nb]
        out = np.zeros((128, NPAIR, 128), f)
        pi = 0
        for j in range(NCT):
            for kt in PAIRS[j]:
                out[:, pi, :] = full[128 * kt:128 * kt + 128, 128 * j:128 * j + 128]
                pi += 1
        return out
    fcw = np.zeros((128, 4, NFT), f)
    fw = np.asarray(inputs["ffn_conv_w"][0], f)
    for i in range(3):
        fcw[:, i, :] = fw[i].reshape(NFT, 128).T
    fcw[:, 3, :] = np.asarray(inputs["ffn_conv_b"][0], f).reshape(NFT, 128).T

    def bc(v):
        return np.ascontiguousarray(np.broadcast_to(np.asarray(v, f).reshape(1, D), (128, D)))
    return {
        "w_in": np.ascontiguousarray(inputs["w_in"][0], f), "w_rnn": np.ascontiguousarray(inputs["w_rnn_out"][0], f),
        "w_att": np.ascontiguousarray(inputs["w_att_out"][0], f), "w_out": np.ascontiguousarray(inputs["w_out"][0], f),
        "w_up": np.ascontiguousarray(inputs["w_up"][0], f), "w_down": np.ascontiguousarray(inputs["w_down"][0], f),
        "g1b": bc(inputs["norm1_g"][0]), "g2b": bc(inputs["norm2_g"][0]), "gfb": bc(inputs["final_g"]),
        "lrup": lrup, "wrd": dense(np.asarray(inputs["lru_wr"][0], f)), "wid": dense(np.asarray(inputs["lru_wi"][0], f)),
        "fcw": fcw, "biash": biash, "biasl": biasl, "ident": np.eye(128, dtype=f),
    }


def core_inputs(x, c, consts):
    b, j = divmod(c, 4)
    s0 = 4096 * j
    lo = s0 - 12288
    xl = np.zeros((NL, D), np.float32)
    if lo < 0:
        xl[-lo:] = x[b, 0:s0 + 4096]
    else:
        xl[:] = x[b, lo:s0 + 4096]
    vf = np.zeros((128, 8), np.float32)
    for cc in range(8):
        vf[:, cc] = 1.0 if 2048 * cc >= 12288 - s0 else 0.0
    m = dict(consts)
    m["xloc"] = xl
    m["vflag"] = vf
    return m


def kernel(**inputs):
    x = np.asarray(inputs["x"], np.float32)
    consts = host_consts(inputs)
    if "nc" not in _CACHE:
        _CACHE["nc"] = build_program()
    nc = _CACHE["nc"]
    in_maps = [core_inputs(x, c, consts) for c in range(8)]
    res = run_bass_kernel_spmd(nc, in_maps, core_ids=list(range(8)))
    out = np.zeros((2, 16384, D), np.float32)
    for c in range(8):
        b, j = divmod(c, 4)
        out[b, 4096 * j:4096 * j + 4096] = res.results[c]["y"]
    return out
```
